# Optimizing a Trainium2 kernel written in Bass

```python
import math
import jax, jax.numpy as jnp
from jax import lax
import numpy as np

D_MODEL = 1024
BATCH = 16
SEQ = 2048
DEPTH = 1

MIX_WIDTH = D_MODEL
SB_WIDTH = D_MODEL // 2
SB_HEADS = 8
SB_HEAD_DIM = SB_WIDTH // SB_HEADS
POOL_WIDTH = MIX_WIDTH - SB_WIDTH
POOL_WINDOWS = (2, 4, 8, 16)
POOL_GROUPS = len(POOL_WINDOWS)
POOL_GROUP_DIM = POOL_WIDTH // POOL_GROUPS
IN_WIDTH = 3 * SB_WIDTH + POOL_WIDTH
D_FF = -(-8 * D_MODEL // (3 * 256)) * 256
Q_BLOCK = 128
N_MOD = 6
EPS = 1e-6

kernel_name = "hybrid_stickbreak_pool_block"


def rmsnorm(x, g):
    xf = x.astype(jnp.float32)
    y = xf * lax.rsqrt(jnp.mean(xf * xf, axis=-1, keepdims=True) + EPS)
    return (y * g.astype(jnp.float32)).astype(x.dtype)


def stick_breaking_attention(q, k, v):
    S = q.shape[2]
    inv_sqrt = 1.0 / math.sqrt(q.shape[-1])
    outs = []
    for i in range(S // Q_BLOCK):
        L = (i + 1) * Q_BLOCK
        qb = q[:, :, i * Q_BLOCK:L].astype(jnp.float32)
        kb = k[:, :, :L].astype(jnp.float32)
        vb = v[:, :, :L].astype(jnp.float32)
        z = jnp.einsum('bhqd,bhkd->bhqk', qb, kb) * inv_sqrt
        t_idx = i * Q_BLOCK + jnp.arange(Q_BLOCK)[:, None]
        s_idx = jnp.arange(L)[None, :]
        mask = s_idx < t_idx
        log1m = jnp.where(mask, -jax.nn.softplus(z), 0.0)
        after = lax.cumsum(log1m, axis=3, reverse=True) - log1m
        logw = jax.nn.log_sigmoid(z) + after
        w = jnp.where(mask, jnp.exp(jnp.where(mask, logw, 0.0)), 0.0)
        outs.append(jnp.einsum('bhqk,bhkd->bhqd', w, vb))
    return jnp.concatenate(outs, axis=2).astype(v.dtype)


def pooling_mixer(u, w_pool, pool_scale):
    B, S, P = u.shape
    uf = u.astype(jnp.float32)
    cs = jnp.concatenate([jnp.zeros((B, 1, P), jnp.float32), jnp.cumsum(uf, axis=1)], axis=1)
    t = jnp.arange(S)
    parts = []
    for g, win in enumerate(POOL_WINDOWS):
        sl = slice(g * POOL_GROUP_DIM, (g + 1) * POOL_GROUP_DIM)
        lo = jnp.maximum(t + 1 - win, 0)
        cnt = (t + 1 - lo).astype(jnp.float32)
        csg = cs[..., sl]
        mean = (csg[:, 1:] - csg[:, lo]) / cnt[None, :, None]
        parts.append(mean - uf[..., sl])
    pooled = jnp.stack(parts, axis=2)
    y = jnp.einsum('bsgc,gcd->bsgd', pooled, w_pool.astype(jnp.float32)).reshape(B, S, P)
    return (y * pool_scale.astype(jnp.float32)).astype(u.dtype)


def setup_inputs(seed: int = 0) -> dict:
    key = jax.random.key(seed)
    ks = jax.random.split(key, 16)
    f32 = jnp.float32

    def nrm(k, shape, fan_in):
        return jax.random.normal(k, shape, f32) * fan_in ** -0.5

    def gain(k):
        return 1.0 + 0.05 * jax.random.normal(k, (DEPTH, D_MODEL), f32)

    return {
        "x": jax.random.normal(ks[0], (BATCH, SEQ, D_MODEL), f32),
        "c": jax.random.normal(ks[1], (BATCH, D_MODEL), f32),
        "w_cond": nrm(ks[2], (DEPTH, D_MODEL, N_MOD * D_MODEL), D_MODEL),
        "b_cond": 0.01 * jax.random.normal(ks[3], (DEPTH, N_MOD * D_MODEL), f32),
        "g_mix_pre": gain(ks[4]),
        "g_mix_post": gain(ks[5]),
        "w_in": nrm(ks[6], (DEPTH, D_MODEL, IN_WIDTH), D_MODEL),
        "w_pool": nrm(ks[7], (DEPTH, POOL_GROUPS, POOL_GROUP_DIM, POOL_GROUP_DIM), POOL_GROUP_DIM),
        "pool_scale": 1.0 + 0.1 * jax.random.normal(ks[8], (DEPTH, POOL_WIDTH), f32),
        "w_out": nrm(ks[9], (DEPTH, MIX_WIDTH, D_MODEL), MIX_WIDTH),
        "g_ffn_pre": gain(ks[10]),
        "g_ffn_post": gain(ks[11]),
        "w_gate": nrm(ks[12], (DEPTH, D_MODEL, D_FF), D_MODEL),
        "w_up": nrm(ks[13], (DEPTH, D_MODEL, D_FF), D_MODEL),
        "w_down": nrm(ks[14], (DEPTH, D_FF, D_MODEL), D_FF),
    }


def reference(x, c, w_cond, b_cond, g_mix_pre, g_mix_post, w_in, w_pool, pool_scale,
              w_out, g_ffn_pre, g_ffn_post, w_gate, w_up, w_down):
    B, S, D = x.shape
    for l in range(DEPTH):
        mod = jax.nn.silu(c) @ w_cond[l] + b_cond[l]
        shift_m, scale_m, gate_m, shift_f, scale_f, gate_f = [
            m[:, None, :] for m in jnp.split(mod, N_MOD, axis=-1)]

        h = rmsnorm(x, g_mix_pre[l]) * (1.0 + scale_m) + shift_m
        proj = h @ w_in[l]
        q, k, v, u = jnp.split(proj, [SB_WIDTH, 2 * SB_WIDTH, 3 * SB_WIDTH], axis=-1)
        to_heads = lambda a: a.reshape(B, S, SB_HEADS, SB_HEAD_DIM).transpose(0, 2, 1, 3)
        attn = stick_breaking_attention(to_heads(q), to_heads(k), to_heads(v))
        attn = attn.transpose(0, 2, 1, 3).reshape(B, S, SB_WIDTH)
        pool = pooling_mixer(u, w_pool[l], pool_scale[l])
        mix = jnp.concatenate([attn, pool], axis=-1) @ w_out[l]
        x = x + gate_m * rmsnorm(mix, g_mix_post[l])

        h = rmsnorm(x, g_ffn_pre[l]) * (1.0 + scale_f) + shift_f
        f = (jax.nn.silu(h @ w_gate[l]) * (h @ w_up[l])) @ w_down[l]
        x = x + gate_f * rmsnorm(f, g_ffn_post[l])
    return x
```

```python
import numpy as np
from contextlib import ExitStack
import concourse.bass as bass
import concourse.mybir as mybir
from concourse.bass_utils import run_bass_kernel_spmd

F32 = mybir.dt.float32
BF16 = mybir.dt.bfloat16
I32 = mybir.dt.int32
AF = mybir.ActivationFunctionType
ALU = mybir.AluOpType

D = 1024
S = 2048
NB = 2
NCORES = 8
DFF = 2816
NF = 22
EPS = 1e-6
NCOLS = 8 * NB + 48 + 8 + 8 + 4
SB_BASE = 16512
SB_END = 229376
NEG = -30000.0
POOL_FRAC = 0.0


class Buf:
    __slots__ = ("name", "w", "r", "excl")

    def __init__(self, name, excl=False):
        self.name = name
        self.w = {}
        self.r = {}
        self.excl = excl


class DSem:
    def __init__(self, sem, key):
        self.sem = sem
        self.count = 0
        self.key = key


class Sched:
    def __init__(self, nc, es):
        self.nc = nc
        self.es = es
        self.eng = dict(pe=nc.tensor, act=nc.scalar, dve=nc.vector, pool=nc.gpsimd, sp=nc.sync)
        self.sem = {k: es.enter_context(nc.semaphore("sem_" + k)) for k in self.eng}
        self.cnt = {k: 0 for k in self.eng}
        self.known = {k: {} for k in self.eng}
        self.nwaits = 0
        self.nds = 0

    def new_dsem(self, name):
        self.nds += 1
        return DSem(self.es.enter_context(self.nc.semaphore(name)), name)

    def _wait(self, e, t):
        sem, val, key = t
        if key == e and e == "pe":
            return
        if self.known[e].get(key, 0) >= val:
            return
        self.eng[e].wait_ge(sem, val)
        self.known[e][key] = val
        self.nwaits += 1

    def begin(self, e, reads=(), writes=(), nowaw=False):
        for b in reads:
            for t in b.w.values():
                self._wait(e, t)
            if b.excl:
                for k, t in b.r.items():
                    if k != e:
                        self._wait(e, t)
        for b in writes:
            if not nowaw:
                for t in b.w.values():
                    self._wait(e, t)
            for t in b.r.values():
                self._wait(e, t)

    def end(self, e, ins, reads=(), writes=()):
        self.cnt[e] += 1
        ins.then_inc(self.sem[e], 1)
        t = (self.sem[e], self.cnt[e], e)
        for b in reads:
            b.r[e] = t
        for b in writes:
            b.w[e] = t
        return t

    def op(self, e, fn, reads=(), writes=(), nowaw=False):
        self.begin(e, reads, writes, nowaw)
        ins = fn(self.eng[e])
        return self.end(e, ins, reads, writes)

    def dma(self, q, out, in_, dsem, reads=(), writes=(), nowaw=False):
        self.begin(q, reads, writes, nowaw)
        ins = self.eng[q].dma_start(out=out, in_=in_)
        dsem.count += 16
        ins.then_inc(dsem.sem, 16)
        t = (dsem.sem, dsem.count, dsem.key)
        for b in reads:
            b.r[dsem.key] = t
        for b in writes:
            b.w[dsem.key] = t
        return t


def build_nc(debug=(), nchunks=4, stop_after=None):
    nc = bass.Bass("TRN2", target_bir_lowering=False)
    es = ExitStack()
    sc = Sched(nc, es)
    dbg_outs = {}

    def din(name, shape):
        return nc.dram_tensor(name, list(shape), F32, kind="ExternalInput").ap()

    x_d = din("x", [NB, S, D])
    cols_d = din("cols", [128, NCOLS])
    rows_d = din("rows", [1, 4096])
    wcond_d = din("w_cond", [D, 6 * D])
    win_d = din("w_in", [D, 2048])
    wpool_d = din("w_pool", [4, 128, 128])
    wout_d = din("w_out", [D, D])
    wgate_d = din("w_gate", [D, DFF])
    wup_d = din("w_up", [D, DFF])
    wdown_d = din("w_down", [DFF, D])
    out_d = nc.dram_tensor("out", [NB, S, D], F32, kind="ExternalOutput").ap()

    off = [SB_BASE]
    ISZ = {F32: 4, BF16: 2, I32: 4}

    def A(name, shape, dt, at=None):
        n = ISZ[dt]
        for d_ in shape[1:]:
            n *= d_
        o = off[0] if at is None else at
        o = (o + 63) // 64 * 64
        t = nc.alloc_sbuf_tensor_at(name, list(shape), dt, offset=o)
        if at is None:
            off[0] = o + n
        assert o + n <= SB_END, (name, o, n)
        return t

    xs = A("xs", [128, 8, 1024], F32)
    HB = A("HB", [128, 8, 1024], BF16)
    kT = A("kT", [128, 4, 2048], BF16)
    vv = A("vv", [128, 16, 512], BF16)
    wsl = A("wsl", [128, 4, 4096], BF16)
    G1 = A("G1", [128, NB, 1024], F32)
    G2 = A("G2", [128, NB, 1024], F32)
    ident = A("ident", [128, 128], BF16)
    maskneg = A("maskneg", [128, 128], BF16)
    onesb = A("onesb", [128, 128], BF16)
    wpool = A("wpool", [128, 4, 128], BF16)
    cols = A("cols_sb", [128, NCOLS], F32)
    AB = A("AB", [128, 4, 8, NB], F32)
    stat = A("stat", [128, 128], F32)
    corr = A("corr", [128, 4, 16], F32)
    corri = A("corri", [128, 16], I32)
    uhalo = A("uhalo", [128, 4, 16], F32)
    sc32 = A("sc32", [128, 8 * NB], F32)
    scT = A("scT", [128, 8 * NB], BF16)
    xn = A("xn", [128, 2, 1024], BF16)
    junk = A("junk", [128, 2, 1024], BF16)
    tmpD = xn[:].bitcast(F32)
    RB = (off[0] + 63) // 64 * 64
    qA = A("qA", [128, 4, 1024], BF16, at=RB)
    qB = A("qB", [128, 4, 1024], BF16, at=RB + 8192)
    R2 = RB + 16384
    uT = A("uT", [128, 4, 1040], F32, at=R2)
    pt1 = A("pt1", [128, 1040], F32, at=R2 + 16640)
    pt2 = A("pt2", [128, 1040], F32, at=R2 + 16640 + 4160)
    pooled = A("pooled", [128, 2, 1024], BF16, at=R2 + 16640 + 8320)
    NGI, NWD, LA = 4, 3, 2
    gI = A("gI", [128, NGI, 2052], F32, at=R2)
    wdf = A("wdf", [128, NWD, 2048], BF16, at=R2 + NGI * 8208 + 16)
    wT = A("wT", [128, 2, 2048], BF16, at=R2 + NGI * 8208 + 16 + NWD * 4096)
    R_END = R2 + NGI * 8208 + 16 + NWD * 4096 + 8192
    print("SBUF: RB", RB, "R_END", R_END, "limit", SB_END)
    rows_bc = A("rows_bc", [128, 4096], F32, at=RB)
    screp = A("screp", [128, NB, 8, 128], BF16, at=RB + 16384)
    actT = A("actT", [128, NF, 1024], BF16, at=RB)
    sg = A("sg", [128, 2, 512], F32, at=RB + 45056)
    assert RB + 45056 + 4096 <= R_END <= SB_END, (RB, R_END)
    stage = HB[:].bitcast(F32)

    pb = [nc.alloc_psum_tensor(f"pb{i}", [128, 512], F32) for i in range(8)]
    pbB = [Buf(f"pb{i}", excl=True) for i in range(8)]

    def slot(i):
        return wsl[:, i, :].rearrange("p (k n) -> p k n", k=8)

    xB = [Buf(f"x{t}") for t in range(8)]
    hB = [Buf(f"h{t}") for t in range(8)]
    kB = [Buf("k0"), Buf("k1")]
    vB = [Buf("v0"), Buf("v1")]
    wB = [Buf(f"ws{i}") for i in range(4)]
    qBf = Buf("q")
    uB = [Buf(f"u{g}") for g in range(4)]
    ptB = [Buf("pt1"), Buf("pt2")]
    poolB = [Buf("pooled0"), Buf("pooled1")]
    gIB = [Buf(f"gI{i}") for i in range(NGI)]
    wdB = [Buf(f"wd{i}") for i in range(NWD)]
    wTB = [Buf("wT0"), Buf("wT1")]
    constB = Buf("const")
    GB = Buf("G")
    ABB = Buf("AB")
    jBs = [Buf(f"junk{i}") for i in range(2)]
    jctr = [0]

    def next_junk():
        i = jctr[0] % 2
        jctr[0] += 1
        return junk[:, i, :], jBs[i]

    xnB = [Buf("xn0"), Buf("xn1")]
    tmpDB = xnB
    uhB = Buf("uhalo")
    actB = Buf("actT")
    sgB = [Buf("sg0"), Buf("sg1")]
    stgB = [Buf(f"stg{t}") for t in range(8)]
    setupB = Buf("setup")


    def fence(new, old):
        for nb_ in new:
            for ob_ in old:
                for dct in (ob_.w, ob_.r):
                    for k_, t_ in dct.items():
                        if k_ not in nb_.r or nb_.r[k_][1] < t_[1]:
                            nb_.r[k_] = t_

    dx = [sc.new_dsem(f"dx{t}") for t in range(8)]
    dw = [sc.new_dsem(f"dw{i}") for i in range(4)]
    dsetup = sc.new_dsem("dsetup")
    dsetup2 = sc.new_dsem("dsetup2")
    ddbgs = []

    def dbg(name, ap, shape, dt, reads):
        if name not in debug or name in dbg_outs:
            return
        t = nc.dram_tensor("dbg_" + name, list(shape), dt, kind="ExternalOutput").ap()
        dbg_outs[name] = t
        dd = sc.new_dsem("ddbg_" + name)
        ddbgs.append(dd)
        sc.dma("sp", t, ap, dd, reads=reads)

    evac_ctr = [0]

    def evac_copy(out, in_, reads, writes, nowaw=True, eng=None):
        e = eng
        if e is None:
            e = "act" if evac_ctr[0] % 2 == 0 else "dve"
            evac_ctr[0] += 1
        if e == "act":
            return sc.op("act", lambda en: en.activation(out=out, in_=in_, func=AF.Copy), reads=reads, writes=writes, nowaw=nowaw)
        return sc.op("dve", lambda en: en.tensor_copy(out=out, in_=in_), reads=reads, writes=writes, nowaw=nowaw)

    def evac_affine(out, in_, scale_ap, bias_ap, reads, writes, eng):
        if eng == "act":
            if bias_ap is None:
                return sc.op("act", lambda en: en.activation(out=out, in_=in_, func=AF.Identity, scale=scale_ap), reads=reads, writes=writes, nowaw=True)
            return sc.op("act", lambda en: en.activation(out=out, in_=in_, func=AF.Identity, scale=scale_ap, bias=bias_ap), reads=reads, writes=writes, nowaw=True)
        if bias_ap is None:
            return sc.op("dve", lambda en: en.tensor_scalar(out=out, in0=in_, scalar1=scale_ap, scalar2=None, op0=ALU.mult), reads=reads, writes=writes, nowaw=True)
        return sc.op("dve", lambda en: en.tensor_scalar(out=out, in0=in_, scalar1=scale_ap, scalar2=bias_ap, op0=ALU.mult, op1=ALU.add), reads=reads, writes=writes, nowaw=True)

    def mm_group(mms, reads, writes, nowaw=False):
        sc.begin("pe", reads, writes, nowaw)
        ins = None
        for (o, l, r, st, sp_) in mms:
            ins = nc.tensor.matmul(o, lhsT=l, rhs=r, start=st, stop=sp_)
        return sc.end("pe", ins, reads, writes)

    def load_w(si, dram_ap, ncols=512):
        sc.dma("pool", slot(si)[:, :, 0:ncols], dram_ap, dw[si], writes=[wB[si]])

    def wview(w_d, c0, c1):
        return w_d.rearrange("(k p) n -> p k n", p=128)[:, :, c0:c1]

    sc.op("pool", lambda e: e.memset(ident[:], 1.0), writes=[constB])
    sc.op("pool", lambda e: e.affine_select(out=ident[:], in_=ident[:], pattern=[[-1, 128]], compare_op=ALU.is_equal,
                                             fill=0.0, base=0, channel_multiplier=1), reads=[constB], writes=[constB])
    sc.op("pool", lambda e: e.memset(maskneg[:], NEG), writes=[constB])
    sc.op("pool", lambda e: e.affine_select(out=maskneg[:], in_=maskneg[:], pattern=[[-1, 128]], compare_op=ALU.is_ge,
                                             fill=0.0, base=0, channel_multiplier=1), reads=[constB], writes=[constB])
    sc.op("pool", lambda e: e.memset(onesb[:], 1.0), writes=[constB])
    sc.op("pool", lambda e: e.memset(uhalo[:], 0.0), writes=[uhB])
    sc.op("pool", lambda e: e.iota(corri[:, 0:15], pattern=[[-1, 15]], base=15, channel_multiplier=0), writes=[constB])
    sc.op("dve", lambda e: e.tensor_copy(out=corr[:, 0, 0:15], in_=corri[:, 0:15]), reads=[constB], writes=[constB])
    for g in range(3, -1, -1):
        win = float(2 << g)
        sc.op("dve", lambda e: e.tensor_scalar(out=corr[:, g, 0:15], in0=corr[:, 0, 0:15], scalar1=win, scalar2=None, op0=ALU.min),
              reads=[constB], writes=[constB])
        sc.op("dve", lambda e: e.reciprocal(out=corr[:, g, 0:15], in_=corr[:, g, 0:15]), reads=[constB], writes=[constB])
        sc.op("dve", lambda e: e.tensor_scalar(out=corr[:, g, 0:15], in0=corr[:, g, 0:15], scalar1=win, scalar2=None, op0=ALU.mult),
              reads=[constB], writes=[constB])

    def finish():
        for t in range(8):
            if dx[t].count:
                nc.sync.wait_ge(dx[t].sem, dx[t].count)
        for d_ in ddbgs + [dsetup, dsetup2]:
            if d_.count:
                nc.sync.wait_ge(d_.sem, d_.count)
        for e_ in ("act", "dve", "pool", "pe"):
            if sc.cnt[e_]:
                nc.sync.wait_ge(sc.sem[e_], sc.cnt[e_])
        return nc, sorted(dbg_outs)

    if stop_after == "s0":
        return finish()
    sc.dma("sp", cols[:], cols_d, dsetup, writes=[setupB])
    sc.dma("sp", rows_bc[:], rows_d[0:1, :].to_broadcast([128, 4096]), dsetup, writes=[setupB])
    sc.dma("pool", wpool[:], wpool_d.rearrange("g c d -> c g d"), dsetup2, writes=[setupB])
    cT = cols[:, 0:8 * NB]
    bcol = cols[:, 8 * NB:8 * NB + 48]
    gpre = [cols[:, 8 * NB + 48:8 * NB + 56], cols[:, 8 * NB + 56:8 * NB + 64]]
    pscale = cols[:, 8 * NB + 64:8 * NB + 68]

    if stop_after == "s1":
        return finish()
    sc.op("act", lambda e: e.activation(out=sc32[:], in_=cT, func=AF.Silu), reads=[setupB], writes=[constB])
    sc.op("dve", lambda e: e.tensor_copy(out=scT[:], in_=sc32[:]), reads=[constB], writes=[constB])
    for b in range(NB):
        for kc in range(8):
            sc.op("dve", lambda e: e.tensor_scalar(out=screp[:, b, kc, :], in0=onesb[:], scalar1=sc32[:, kc * NB + b:kc * NB + b + 1],
                                                    scalar2=None, op0=ALU.mult), reads=[constB], writes=[setupB], nowaw=True)

    if stop_after == "s2":
        return finish()
    colslots = {0: (0, 0), 1: (0, 4), 2: (1, 0), 3: (1, 4), 6: (2, 0), 7: (2, 4), 8: (3, 0), 9: (3, 4)}
    rowslots = {4: (G1, 0, 0), 5: (G1, 0, 1), 10: (G2, 1, 0), 11: (G2, 1, 1)}
    modps = pb[7]
    for j in range(12):
        si = j % 4
        load_w(si, wview(wcond_d, j * 512, (j + 1) * 512))
        if j in colslots:
            v, kc0 = colslots[j]
            for c4 in range(4):
                ci = (v * 8 + kc0 + c4) * NB
                mm_group([(modps[:, ci:ci + NB], slot(si)[:, kc, c4 * 128:(c4 + 1) * 128], scT[:, kc * NB:(kc + 1) * NB], kc == 0, kc == 7)
                          for kc in range(8)], reads=[wB[si], constB], writes=[pbB[7]], nowaw=True)
        else:
            Gt, gi_, half = rowslots[j]
            for b in range(NB):
                bank = (j + b) % 4
                mm_group([(pb[bank][:, :], screp[:, b, kc, :], slot(si)[:, kc, :], kc == 0, kc == 7) for kc in range(8)],
                         reads=[wB[si], setupB], writes=[pbB[bank]])
                sc.op("dve", lambda e: e.tensor_tensor(out=Gt[:, b, half * 512:(half + 1) * 512], in0=pb[bank][:, :],
                                                        in1=rows_bc[:, gi_ * 1024 + half * 512:gi_ * 1024 + (half + 1) * 512], op=ALU.add),
                      reads=[pbB[bank], setupB], writes=[GB], nowaw=True)
                sc.op("dve", lambda e: e.tensor_tensor(out=Gt[:, b, half * 512:(half + 1) * 512], in0=Gt[:, b, half * 512:(half + 1) * 512],
                                                        in1=rows_bc[:, 2048 + gi_ * 1024 + half * 512:2048 + gi_ * 1024 + (half + 1) * 512], op=ALU.mult),
                      reads=[GB, setupB], writes=[GB])
    if stop_after == "s3":
        return finish()
    bchunk = {0: 0, 1: 8, 2: 24, 3: 32}
    mp = modps[:, 0:32 * NB].rearrange("p (v k b) -> p v k b", v=4, k=8)
    for b in range(NB):
        for v in range(4):
            sc.op("dve", lambda e: e.tensor_tensor(out=AB[:, v, :, b], in0=mp[:, v, :, b], in1=bcol[:, bchunk[v]:bchunk[v] + 8], op=ALU.add),
                  reads=[pbB[7], setupB], writes=[ABB], nowaw=True)
        for v, gp in ((1, gpre[0]), (3, gpre[1])):
            sc.op("dve", lambda e: e.scalar_tensor_tensor(out=AB[:, v, :, b], in0=AB[:, v, :, b], scalar=1.0, in1=gp, op0=ALU.add, op1=ALU.mult),
                  reads=[ABB, setupB], writes=[ABB])
    dbg("AB", AB[:], [128, 4, 8, NB], F32, [ABB])
    dbg("G1", G1[:], [128, NB, 1024], F32, [GB])
    dbg("G2", G2[:], [128, NB, 1024], F32, [GB])
    fence([qBf] + uB + ptB + poolB, [setupB])

    bank_rr = [0]

    def next_bank(choices):
        b = choices[bank_rr[0] % len(choices)]
        bank_rr[0] += 1
        return b

    def rstd_chain(ssq_aps, tmp_ap, out_ap, rbufs, tbuf, obuf):
        sc.op("dve", lambda e: e.tensor_scalar(out=tmp_ap, in0=ssq_aps[0], scalar1=1.0 / D, scalar2=EPS, op0=ALU.mult, op1=ALU.add),
              reads=rbufs, writes=[tbuf])
        for extra in ssq_aps[1:]:
            sc.op("dve", lambda e: e.scalar_tensor_tensor(out=tmp_ap, in0=extra, scalar=1.0 / D, in1=tmp_ap, op0=ALU.mult, op1=ALU.add),
                  reads=rbufs + [tbuf], writes=[tbuf])
        sc.op("act", lambda e: e.activation(out=tmp_ap, in_=tmp_ap, func=AF.Sqrt), reads=[tbuf], writes=[tbuf])
        sc.op("dve", lambda e: e.reciprocal(out=out_ap, in_=tmp_ap), reads=[tbuf], writes=[obuf])

    def norm_transpose(t, b, rstd_ap, rbuf, vA, vB_, slot_i, all_act=False):
        xsl = slot_i % 2
        if all_act:
            sc.op("act", lambda e: e.activation(out=xn[:, xsl, :], in_=xs[:, t, :], func=AF.Copy, scale=rstd_ap),
                  reads=[xB[t], rbuf], writes=[xnB[xsl]])
        else:
            sc.op("dve", lambda e: e.tensor_scalar(out=xn[:, xsl, :], in0=xs[:, t, :], scalar1=rstd_ap, scalar2=None, op0=ALU.mult),
                  reads=[xB[t], rbuf], writes=[xnB[xsl]])
        bank = 3 + (slot_i % 2)
        psT = pb[bank][:].bitcast(BF16)
        sc.begin("pe", [xnB[xsl], constB], [pbB[bank]])
        ins = None
        for kc in range(8):
            ins = nc.tensor.transpose(psT[:, kc * 128:(kc + 1) * 128], xn[:, xsl, kc * 128:(kc + 1) * 128], ident[:])
        sc.end("pe", ins, [xnB[xsl], constB], [pbB[bank]])
        for kc in range(8):
            evac_affine(HB[:, kc, t * 128:(t + 1) * 128], psT[:, kc * 128:(kc + 1) * 128], AB[:, vA, kc, b:b + 1], AB[:, vB_, kc, b:b + 1],
                        reads=[pbB[bank], ABB], writes=[hB[t]], eng="act" if (all_act or slot_i % 2 == 0) else "dve")

    SSQA = stat[:, 0:8]
    TMPA = stat[:, 8:16]
    RSTDA = stat[:, 16:24]
    ssqABs = [Buf(f"ssqA{t}") for t in range(8)]
    tmpAB, rstdAB = Buf("tmpA"), Buf("rstdA")
    SSQD = stat[:, 24:40]
    TMPD = stat[:, 40:48]
    RSTDD = stat[:, 48:56]
    SSQ2 = stat[:, 56:64]
    TMP2 = stat[:, 64:72]
    RSTD2 = stat[:, 72:80]
    SSQF = stat[:, 80:96]
    TMPF = stat[:, 96:104]
    RSTDF = stat[:, 104:112]
    ssqDB = [[Buf(f"ssqD{t}_{c}") for c in range(2)] for t in range(8)]
    tmpDsB = [Buf(f"tmpDs{t}") for t in range(8)]
    rstdDB = [Buf(f"rstdD{t}") for t in range(8)]
    ssq2B = [Buf(f"ssq2{t}") for t in range(8)]
    tmp2B = [Buf(f"tmp2{t}") for t in range(8)]
    rstd2B = [Buf(f"rstd2{t}") for t in range(8)]
    ssqFB = [[Buf(f"ssqF{t}_{c}") for c in range(2)] for t in range(8)]
    tmpFB = [Buf(f"tmpF{t}") for t in range(8)]
    rstdFB = [Buf(f"rstdF{t}") for t in range(8)]

    chunks = [(0, 1), (0, 0), (1, 1), (1, 0)][:nchunks]
    if stop_after == "setup":
        chunks = []
    for ci, (s, half) in enumerate(chunks):
        b = s
        r0 = half * 1024
        gt0 = r0 // 128
        first = (ci == 0)

        fence(hB, stgB)
        fence([qBf] + uB + ptB + poolB, [actB] + sgB + tmpDB)
        for t in range(8):
            sc.dma("sp", xs[:, t, :], x_d[s, r0 + t * 128:r0 + (t + 1) * 128, :], dx[t], writes=[xB[t]])
        for si in range(4):
            load_w(si, wview(win_d, si * 512, (si + 1) * 512))
        if stop_after == "A0":
            continue
        for t in range(8):
            jk, jb_ = next_junk()
            sc.op("act", lambda e: e.activation(out=jk, in_=xs[:, t, :], func=AF.Square, accum_out=SSQA[:, t:t + 1]),
                  reads=[xB[t]], writes=[jb_, ssqABs[t]])
        if stop_after == "A1":
            continue
        rstd_chain([SSQA], TMPA, RSTDA, ssqABs, tmpAB, rstdAB)
        if stop_after == "A2":
            continue
        for t in range(8):
            norm_transpose(t, b, RSTDA[:, t:t + 1], rstdAB, 1, 0, t)
        if first:
            dbg("hT", HB[:], [128, 8, 1024], BF16, hB)

        if stop_after == "A":
            continue
        pbanks = [0, 1, 2, 5, 6]
        sc.op("pool", lambda e: e.memset(qA[64:128, :, :], 0.0), writes=[qBf])
        sc.op("pool", lambda e: e.memset(qB[0:64, :, :], 0.0), writes=[qBf], nowaw=True)
        for j in range(4):
            for tc in range(2):
                bank = next_bank(pbanks)
                mm_group([(pb[bank][:, :], slot(0)[:, kc, j * 128:(j + 1) * 128], HB[:, kc, tc * 512:(tc + 1) * 512], kc == 0, kc == 7)
                          for kc in range(8)], reads=[wB[0]] + hB[tc * 4:(tc + 1) * 4], writes=[pbB[bank]])
                qe = "act" if (j * 2 + tc) % 2 == 0 else "dve"
                evac_copy(qA[0:64, j, tc * 512:(tc + 1) * 512], pb[bank][0:64, :], reads=[pbB[bank]], writes=[qBf], eng=qe)
                evac_copy(qB[64:128, j, tc * 512:(tc + 1) * 512], pb[bank][64:128, :], reads=[pbB[bank]], writes=[qBf], eng=qe)
        for j in range(4):
            for tc in range(2):
                bank = next_bank(pbanks)
                mm_group([(pb[bank][:, :], slot(1)[:, kc, j * 128:(j + 1) * 128], HB[:, kc, tc * 512:(tc + 1) * 512], kc == 0, kc == 7)
                          for kc in range(8)], reads=[wB[1]] + hB[tc * 4:(tc + 1) * 4], writes=[pbB[bank]])
                evac_copy(kT[:, j, r0 + tc * 512:r0 + (tc + 1) * 512], pb[bank][:, :], reads=[pbB[bank]], writes=[kB[half]])
        for t in range(8):
            bank = next_bank(pbanks)
            mm_group([(pb[bank][:, :], HB[:, kc, t * 128:(t + 1) * 128], slot(2)[:, kc, :], kc == 0, kc == 7) for kc in range(8)],
                     reads=[wB[2], hB[t]], writes=[pbB[bank]])
            evac_copy(vv[:, gt0 + t, :], pb[bank][:, :], reads=[pbB[bank]], writes=[vB[half]])
        for g in range(4):
            for tc in range(2):
                bank = next_bank(pbanks)
                mm_group([(pb[bank][:, :], slot(3)[:, kc, g * 128:(g + 1) * 128], HB[:, kc, tc * 512:(tc + 1) * 512], kc == 0, kc == 7)
                          for kc in range(8)], reads=[wB[3]] + hB[tc * 4:(tc + 1) * 4], writes=[pbB[bank]])
                evac_copy(uT[:, g, tc * 512:(tc + 1) * 512], pb[bank][:, :], reads=[pbB[bank]], writes=[uB[g]])
        if first:
            dbg("qA", qA[:], [128, 4, 1024], BF16, [qBf])
            dbg("qB", qB[:], [128, 4, 1024], BF16, [qBf])
            dbg("kT", kT[:, :, 1024:2048], [128, 4, 1024], BF16, kB)
            dbg("vv", vv[:, 8:16, :], [128, 8, 512], BF16, vB)
            dbg("uT", uT[:, :, 0:1024], [128, 4, 1024], F32, uB)
        for chh in range(2):
            load_w(chh, wview(wout_d, chh * 512, (chh + 1) * 512))

        if stop_after == "B":
            continue
        for g in range(4):
            win = 2 << g
            if half == 1:
                sc.op("dve", lambda e: e.memset(uT[:, g, 1024:1040], 0.0), writes=[uB[g]], nowaw=True)
                sc.op("dve", lambda e: e.tensor_copy(out=uhalo[:, g, :], in_=uT[:, g, 0:16]), reads=[uB[g]], writes=[uhB], nowaw=True)
            else:
                sc.op("dve", lambda e: e.tensor_copy(out=uT[:, g, 1024:1040], in_=uhalo[:, g, :]), reads=[uhB], writes=[uB[g]], nowaw=True)
            src, srcB = uT[:, g, :], uB[g]
            n, sh, k = 1040, 1, 0
            while sh < win:
                n -= sh
                dst, dstB = (pt1, ptB[0]) if k % 2 == 0 else (pt2, ptB[1])
                sc.op("dve", lambda e: e.tensor_tensor(out=dst[:, 0:n], in0=src[:, 0:n], in1=src[:, sh:sh + n], op=ALU.add),
                      reads=[srcB], writes=[dstB])
                src, srcB = dst[:, :], dstB
                sh *= 2
                k += 1
            if half == 1:
                sc.op("dve", lambda e: e.tensor_tensor(out=src[:, 1009:1024], in0=src[:, 1009:1024], in1=corr[:, g, 0:15], op=ALU.mult),
                      reads=[srcB, constB], writes=[srcB])
            ps_ = g % 2
            sc.op("dve", lambda e: e.scalar_tensor_tensor(out=pooled[:, ps_, :], in0=src[:, 0:1024], scalar=1.0 / win, in1=uT[:, g, 0:1024],
                                                           op0=ALU.mult, op1=ALU.subtract), reads=[srcB, uB[g]], writes=[poolB[ps_]])
            for tc in range(2):
                bank = next_bank(pbanks)
                mm_group([(pb[bank][:, :], wpool[:, g, :], pooled[:, ps_, tc * 512:(tc + 1) * 512], True, True)],
                         reads=[setupB, poolB[ps_]], writes=[pbB[bank]])
                evac_affine(HB[:, 4 + g, tc * 512:(tc + 1) * 512], pb[bank][:, :], pscale[:, g:g + 1], None,
                            reads=[pbB[bank], setupB], writes=hB[tc * 4:(tc + 1) * 4], eng="act" if tc == 0 else "dve")

        if stop_after == "P":
            continue
        items = [(qb, h) for qb in range(8) for h in range(8)]
        zbanks = [0, 1]
        zc = [0]

        def att_front(n):
            qb, h = items[n]
            gi = gt0 + qb
            nkb = 16 - gi
            L = nkb * 128
            sl = n % NGI
            qsrc = qA if h % 2 == 0 else qB
            j = h // 2
            nch = (L + 511) // 512
            for c in range(nch):
                ncl = min(512, L - c * 512)
                bank = zbanks[zc[0] % 2]
                zc[0] += 1
                mms = [(pb[bank][:, 0:ncl], qsrc[:, j, qb * 128:(qb + 1) * 128], kT[:, j, gi * 128 + c * 512:gi * 128 + c * 512 + ncl], True, c != 0)]
                if c == 0:
                    mms.append((pb[bank][:, 0:128], ident[:], maskneg[:], False, True))
                mm_group(mms, reads=[qBf, kB[0], kB[1], constB], writes=[pbB[bank]])
                sc.op("act", lambda e: e.activation(out=gI[:, sl, 1 + c * 512:1 + c * 512 + ncl], in_=pb[bank][:, 0:ncl], func=AF.Sigmoid, scale=-0.125),
                      reads=[pbB[bank]], writes=[gIB[sl]], nowaw=(c != 0))
            sc.op("dve", lambda e: e.tensor_tensor_scan(out=gI[:, sl, 1:L + 1], data0=gI[:, sl, 1:L + 1], data1=gI[:, sl, 1:L + 1],
                                                         initial=1.0, op0=ALU.mult, op1=ALU.min), reads=[gIB[sl]], writes=[gIB[sl]])
            sw = n % NWD
            Lp = (int(L * POOL_FRAC) // 2) * 2
            if Lp > 0:
                sc.op("pool", lambda e: e.tensor_tensor(out=wdf[:, sw, 0:Lp], in0=gI[:, sl, 0:Lp], in1=gI[:, sl, 1:Lp + 1], op=ALU.subtract),
                      reads=[gIB[sl]], writes=[wdB[sw]])
            sc.op("dve", lambda e: e.tensor_tensor(out=wdf[:, sw, Lp:L], in0=gI[:, sl, Lp:L], in1=gI[:, sl, Lp + 1:L + 1], op=ALU.subtract),
                  reads=[gIB[sl]], writes=[wdB[sw]], nowaw=(Lp > 0))

        def att_back(n):
            qb, h = items[n]
            gi = gt0 + qb
            nkb = 16 - gi
            sw = n % NWD
            st = n % 2
            j = h // 2
            kb = 0
            grp = 0
            while kb < nkb:
                ng = min(8, nkb - kb)
                bank = 3 + (grp + n) % 2
                psT = pb[bank][:].bitcast(BF16)
                sc.begin("pe", [wdB[sw], constB], [pbB[bank]])
                ins = None
                for i in range(ng):
                    ins = nc.tensor.transpose(psT[:, i * 128:(i + 1) * 128], wdf[:, sw, (kb + i) * 128:(kb + i + 1) * 128], ident[:])
                sc.end("pe", ins, [wdB[sw], constB], [pbB[bank]])
                sc.op("act", lambda e: e.activation(out=wT[:, st, kb * 128:(kb + ng) * 128], in_=psT[:, 0:ng * 128], func=AF.Copy),
                      reads=[pbB[bank]], writes=[wTB[st]], nowaw=(kb != 0))
                kb += ng
                grp += 1
            bank = 5 + (n % 2)
            mm_group([(pb[bank][:, 0:128], vv[:, gi + k2, j * 128:(j + 1) * 128], wT[:, st, k2 * 128:(k2 + 1) * 128], k2 == 0, k2 == nkb - 1)
                      for k2 in range(nkb)], reads=[vB[0], vB[1], wTB[st]], writes=[pbB[bank]])
            ph = (h % 2) * 64
            evac_copy(HB[ph:ph + 64, j, qb * 128:(qb + 1) * 128], pb[bank][ph:ph + 64, 0:128], reads=[pbB[bank]], writes=[hB[qb]],
                      eng="dve" if h % 2 == 0 else "act")

        def d_tile(t):
            banks = (2, 7)
            for chh in range(2):
                mm_group([(pb[banks[chh]][:, :], HB[:, kc, t * 128:(t + 1) * 128], slot(chh)[:, kc, :], kc == 0, kc == 7) for kc in range(8)],
                         reads=[wB[chh], hB[t]], writes=[pbB[banks[chh]]])
                jk, jb_ = next_junk()
                sc.op("act", lambda e: e.activation(out=jk[:, 0:512], in_=pb[banks[chh]][:, :], func=AF.Square,
                                                     accum_out=SSQD[:, 2 * t + chh:2 * t + chh + 1]),
                      reads=[pbB[banks[chh]]], writes=[jb_, ssqDB[t][chh]])
            rstd_chain([SSQD[:, 2 * t:2 * t + 1], SSQD[:, 2 * t + 1:2 * t + 2]], TMPD[:, t:t + 1], RSTDD[:, t:t + 1], ssqDB[t], tmpDsB[t], rstdDB[t])
            for chh in range(2):
                ts_ = chh
                sc.op("dve", lambda e: e.scalar_tensor_tensor(out=tmpD[:, ts_, :], in0=pb[banks[chh]][:, :], scalar=RSTDD[:, t:t + 1],
                                                               in1=G1[:, b, chh * 512:(chh + 1) * 512], op0=ALU.mult, op1=ALU.mult),
                      reads=[pbB[banks[chh]], rstdDB[t], GB], writes=[tmpDB[ts_]])
                sc.op("pool", lambda e: e.tensor_tensor(out=xs[:, t, chh * 512:(chh + 1) * 512], in0=xs[:, t, chh * 512:(chh + 1) * 512],
                                                         in1=tmpD[:, ts_, :], op=ALU.add), reads=[tmpDB[ts_], xB[t]], writes=[xB[t]])
            jk, jb_ = next_junk()
            sc.op("act", lambda e: e.activation(out=jk, in_=xs[:, t, :], func=AF.Square, accum_out=SSQ2[:, t:t + 1]),
                  reads=[xB[t]], writes=[jb_, ssq2B[t]])
            rstd_chain([SSQ2[:, t:t + 1]], TMP2[:, t:t + 1], RSTD2[:, t:t + 1], [ssq2B[t]], tmp2B[t], rstd2B[t])
            norm_transpose(t, b, RSTD2[:, t:t + 1], rstd2B[t], 3, 2, t, all_act=True)

        fence(gIB + wdB + wTB, uB + ptB + poolB)
        for sl in range(NGI):
            sc.op("dve", lambda e: e.memset(gI[:, sl, 0:1], 1.0), reads=[], writes=[gIB[sl]])
        NI = len(items)
        for n in range(NI + LA):
            if n < NI:
                att_front(n)
            if n >= LA:
                att_back(n - LA)
                if (n - LA) % 8 == 7:
                    d_tile((n - LA) // 8)

        if stop_after == "C":
            continue
        load_w(2, wview(wgate_d, 0, 512))
        load_w(3, wview(wup_d, 0, 512))

        if first:
            dbg("x1", xs[:], [128, 8, 1024], F32, xB)
            dbg("h2T", HB[:], [128, 8, 1024], BF16, hB)

        if stop_after == "D":
            continue
        fence([actB] + sgB, [qBf] + gIB + wdB + wTB + tmpDB + uB + ptB + poolB)
        nfs = 6
        gu_banks = [(0, 1), (2, 5)]
        it = 0
        for fs in range(nfs):
            sg_, su_ = (2, 3) if fs % 2 == 0 else (0, 1)
            nfl = min(4, NF - fs * 4)
            if fs + 1 < nfs:
                ng_, nu_ = (0, 1) if fs % 2 == 0 else (2, 3)
                c0 = (fs + 1) * 512
                c1 = min(DFF, c0 + 512)
                load_w(ng_, wview(wgate_d, c0, c1), c1 - c0)
                load_w(nu_, wview(wup_d, c0, c1), c1 - c0)
            for f4 in range(nfl):
                f = fs * 4 + f4
                for tc in range(2):
                    gb_, ub_ = gu_banks[it % 2]
                    sgs = it % 2
                    it += 1
                    mm_group([(pb[gb_][:, :], slot(sg_)[:, kc, f4 * 128:(f4 + 1) * 128], HB[:, kc, tc * 512:(tc + 1) * 512], kc == 0, kc == 7)
                              for kc in range(8)], reads=[wB[sg_]] + hB[tc * 4:(tc + 1) * 4], writes=[pbB[gb_]])
                    mm_group([(pb[ub_][:, :], slot(su_)[:, kc, f4 * 128:(f4 + 1) * 128], HB[:, kc, tc * 512:(tc + 1) * 512], kc == 0, kc == 7)
                              for kc in range(8)], reads=[wB[su_]] + hB[tc * 4:(tc + 1) * 4], writes=[pbB[ub_]])
                    sc.op("act", lambda e: e.activation(out=sg[:, sgs, :], in_=pb[gb_][:, :], func=AF.Silu), reads=[pbB[gb_]], writes=[sgB[sgs]])
                    sc.op("dve", lambda e: e.tensor_tensor(out=actT[:, f, tc * 512:(tc + 1) * 512], in0=sg[:, sgs, :], in1=pb[ub_][:, :], op=ALU.mult),
                          reads=[sgB[sgs], pbB[ub_]], writes=[actB], nowaw=True)
        if first:
            dbg("actT", actT[:], [128, NF, 1024], BF16, [actB])

        if stop_after == "E":
            continue
        wdv = wdown_d.rearrange("(f p) n -> p f n", p=128)
        dslots = [(0, 8), (8, 8), (16, 6)]
        useq = 0
        for chh in range(2):
            for di, (f0, nfd) in enumerate(dslots):
                si = useq % 4
                useq += 1
                sc.dma("pool", slot(si)[:, 0:nfd, :], wdv[:, f0:f0 + nfd, chh * 512:(chh + 1) * 512], dw[si], writes=[wB[si]])
                for f8 in range(nfd):
                    f = f0 + f8
                    for t in range(8):
                        mm_group([(pb[t][:, :], actT[:, f, t * 128:(t + 1) * 128], slot(si)[:, f8, :], f == 0, f == NF - 1)],
                                 reads=[wB[si], actB], writes=[pbB[t]])
            for t in range(8):
                jk, jb_ = next_junk()
                sc.op("act", lambda e: e.activation(out=jk[:, 0:512], in_=pb[t][:, :], func=AF.Square,
                                                     accum_out=SSQF[:, 2 * t + chh:2 * t + chh + 1]),
                      reads=[pbB[t]], writes=[jb_, ssqFB[t][chh]])
                if chh == 0:
                    sc.op("dve", lambda e: e.tensor_copy(out=stage[:, t, :], in_=pb[t][:, :]),
                          reads=[pbB[t]], writes=[stgB[t]] + hB)
            if chh == 1:
                sq2 = SSQF.rearrange("p (t c) -> p t c", c=2)
                rstd_chain([sq2[:, :, 0], sq2[:, :, 1]], TMPF, RSTDF, [x_ for p_ in ssqFB for x_ in p_], tmpFB[0], rstdFB[0])
                for t in range(8):
                    for c2 in range(2):
                        src_ap = stage[:, t, :] if c2 == 0 else pb[t][:, :]
                        srcbuf = stgB[t] if c2 == 0 else pbB[t]
                        ts_ = c2
                        sc.op("dve", lambda e: e.scalar_tensor_tensor(out=tmpD[:, ts_, :], in0=src_ap, scalar=RSTDF[:, t:t + 1],
                                                                       in1=G2[:, b, c2 * 512:(c2 + 1) * 512], op0=ALU.mult, op1=ALU.mult),
                              reads=[srcbuf, rstdFB[0], GB], writes=[tmpDB[ts_]])
                        sc.op("pool", lambda e: e.tensor_tensor(out=xs[:, t, c2 * 512:(c2 + 1) * 512], in0=xs[:, t, c2 * 512:(c2 + 1) * 512],
                                                                 in1=tmpD[:, ts_, :], op=ALU.add), reads=[tmpDB[ts_], xB[t]], writes=[xB[t]])
                    sc.dma("sp", out_d[s, r0 + t * 128:r0 + (t + 1) * 128, :], xs[:, t, :], dx[t], reads=[xB[t]])

    return finish()


def make_in_maps(x, c, w_cond, b_cond, g_mix_pre, g_mix_post, w_in, w_pool, pool_scale, w_out,
                 g_ffn_pre, g_ffn_post, w_gate, w_up, w_down):
    f = lambda a: np.ascontiguousarray(np.asarray(a, dtype=np.float32))
    x = f(x)
    c = f(c)
    xr = x[:, ::-1, :]
    bc = f(b_cond)[0]
    colv = lambda v: v.reshape(-1, 128).T
    rows = np.concatenate([bc[2048:3072], bc[5120:6144], f(g_mix_post)[0], f(g_ffn_post)[0]])[None, :]
    shared = dict(rows=f(rows), w_cond=f(w_cond)[0], w_in=f(w_in)[0], w_pool=f(w_pool)[0], w_out=f(w_out)[0],
                  w_gate=f(w_gate)[0], w_up=f(w_up)[0], w_down=f(w_down)[0])
    maps = []
    for i in range(NCORES):
        cb = c[i * NB:(i + 1) * NB]
        cT = cb.reshape(NB, 8, 128).transpose(2, 1, 0).reshape(128, 8 * NB)
        cols = np.concatenate([cT, colv(bc), colv(f(g_mix_pre)[0]), colv(f(g_ffn_pre)[0]), colv(f(pool_scale)[0])], axis=1)
        m = dict(shared)
        m["x"] = f(xr[i * NB:(i + 1) * NB])
        m["cols"] = f(cols)
        maps.append(m)
    return maps


_NC_CACHE = {}


def kernel(**inputs):
    if "nc" not in _NC_CACHE:
        _NC_CACHE["nc"] = build_nc()[0]
    nc = _NC_CACHE["nc"]
    maps = make_in_maps(**inputs)
    res = run_bass_kernel_spmd(nc, maps, core_ids=list(range(NCORES)))
    outs = [np.asarray(r["out"]) for r in res.results]
    full = np.concatenate(outs, axis=0)[:, ::-1, :]
    return np.ascontiguousarray(full.astype(np.float32))
```

```python
import numpy as np
from contextlib import ExitStack
import concourse.bass as bass
import concourse.mybir as mybir
from concourse.bass_utils import run_bass_kernel_spmd

F32 = mybir.dt.float32
BF16 = mybir.dt.bfloat16
I32 = mybir.dt.int32
AF = mybir.ActivationFunctionType
ALU = mybir.AluOpType

D = 1024
S = 2048
NB = 2
NCORES = 8
DFF = 2816
NF = 22
EPS = 1e-6
NCOLS = 8 * NB + 48 + 8 + 8 + 4
SB_BASE = 16512
SB_END = 229376
NEG = -30000.0
POOL_FRAC = 0.0


class Buf:
    __slots__ = ("name", "w", "r", "excl")

    def __init__(self, name, excl=False):
        self.name = name
        self.w = {}
        self.r = {}
        self.excl = excl


class DSem:
    def __init__(self, sem, key):
        self.sem = sem
        self.count = 0
        self.key = key


class Sched:
    def __init__(self, nc, es):
        self.nc = nc
        self.es = es
        self.eng = dict(pe=nc.tensor, act=nc.scalar, dve=nc.vector, pool=nc.gpsimd, sp=nc.sync)
        self.sem = {k: es.enter_context(nc.semaphore("sem_" + k)) for k in self.eng}
        self.cnt = {k: 0 for k in self.eng}
        self.known = {k: {} for k in self.eng}
        self.nwaits = 0
        self.nds = 0

    def new_dsem(self, name):
        self.nds += 1
        return DSem(self.es.enter_context(self.nc.semaphore(name)), name)

    def _wait(self, e, t):
        sem, val, key = t
        if key == e and e == "pe":
            return
        if self.known[e].get(key, 0) >= val:
            return
        self.eng[e].wait_ge(sem, val)
        self.known[e][key] = val
        self.nwaits += 1

    def begin(self, e, reads=(), writes=(), nowaw=False):
        for b in reads:
            for t in b.w.values():
                self._wait(e, t)
            if b.excl:
                for k, t in b.r.items():
                    if k != e:
                        self._wait(e, t)
        for b in writes:
            if not nowaw:
                for t in b.w.values():
                    self._wait(e, t)
            for t in b.r.values():
                self._wait(e, t)

    def end(self, e, ins, reads=(), writes=()):
        self.cnt[e] += 1
        ins.then_inc(self.sem[e], 1)
        t = (self.sem[e], self.cnt[e], e)
        for b in reads:
            b.r[e] = t
        for b in writes:
            b.w[e] = t
        return t

    def op(self, e, fn, reads=(), writes=(), nowaw=False):
        self.begin(e, reads, writes, nowaw)
        ins = fn(self.eng[e])
        return self.end(e, ins, reads, writes)

    def dma(self, q, out, in_, dsem, reads=(), writes=(), nowaw=False):
        self.begin(q, reads, writes, nowaw)
        ins = self.eng[q].dma_start(out=out, in_=in_)
        dsem.count += 16
        ins.then_inc(dsem.sem, 16)
        t = (dsem.sem, dsem.count, dsem.key)
        for b in reads:
            b.r[dsem.key] = t
        for b in writes:
            b.w[dsem.key] = t
        return t


def build_nc(debug=(), nchunks=4, stop_after=None):
    nc = bass.Bass("TRN2", target_bir_lowering=False)
    es = ExitStack()
    sc = Sched(nc, es)
    dbg_outs = {}

    def din(name, shape):
        return nc.dram_tensor(name, list(shape), F32, kind="ExternalInput").ap()

    x_d = din("x", [NB, S, D])
    cols_d = din("cols", [128, NCOLS])
    rows_d = din("rows", [1, 4096])
    wcond_d = din("w_cond", [D, 6 * D])
    win_d = din("w_in", [D, 2048])
    wpool_d = din("w_pool", [4, 128, 128])
    wout_d = din("w_out", [D, D])
    wgate_d = din("w_gate", [D, DFF])
    wup_d = din("w_up", [D, DFF])
    wdown_d = din("w_down", [DFF, D])
    out_d = nc.dram_tensor("out", [NB, S, D], F32, kind="ExternalOutput").ap()

    off = [SB_BASE]
    ISZ = {F32: 4, BF16: 2, I32: 4}

    def A(name, shape, dt, at=None):
        n = ISZ[dt]
        for d_ in shape[1:]:
            n *= d_
        o = off[0] if at is None else at
        o = (o + 63) // 64 * 64
        t = nc.alloc_sbuf_tensor_at(name, list(shape), dt, offset=o)
        if at is None:
            off[0] = o + n
        assert o + n <= SB_END, (name, o, n)
        return t

    xs = A("xs", [128, 8, 1024], F32)
    HB = A("HB", [128, 8, 1024], BF16)
    kT = A("kT", [128, 4, 2048], BF16)
    vv = A("vv", [128, 16, 512], BF16)
    wsl = A("wsl", [128, 4, 4096], BF16)
    G1 = A("G1", [128, NB, 1024], F32)
    G2 = A("G2", [128, NB, 1024], F32)
    ident = A("ident", [128, 128], BF16)
    maskneg = A("maskneg", [128, 128], BF16)
    onesb = A("onesb", [128, 128], BF16)
    wpool = A("wpool", [128, 4, 128], BF16)
    cols = A("cols_sb", [128, NCOLS], F32)
    AB = A("AB", [128, 4, 8, NB], F32)
    stat = A("stat", [128, 128], F32)
    corr = A("corr", [128, 4, 16], F32)
    corri = A("corri", [128, 16], I32)
    uhalo = A("uhalo", [128, 4, 16], F32)
    sc32 = A("sc32", [128, 8 * NB], F32)
    scT = A("scT", [128, 8 * NB], BF16)
    xn = A("xn", [128, 2, 1024], BF16)
    junk = A("junk", [128, 2, 1024], BF16)
    tmpD = xn[:].bitcast(F32)
    RB = (off[0] + 63) // 64 * 64
    qA = A("qA", [128, 4, 1024], BF16, at=RB)
    qB = A("qB", [128, 4, 1024], BF16, at=RB + 8192)
    R2 = RB + 16384
    uT = A("uT", [128, 4, 1040], F32, at=R2)
    pt1 = A("pt1", [128, 1040], F32, at=R2 + 16640)
    pt2 = A("pt2", [128, 1040], F32, at=R2 + 16640 + 4160)
    pooled = A("pooled", [128, 2, 1024], BF16, at=R2 + 16640 + 8320)
    NGI, NWD, LA = 4, 3, 2
    gI = A("gI", [128, NGI, 2052], F32, at=R2)
    wdf = A("wdf", [128, NWD, 2048], BF16, at=R2 + NGI * 8208 + 16)
    wT = A("wT", [128, 2, 2048], BF16, at=R2 + NGI * 8208 + 16 + NWD * 4096)
    R_END = R2 + NGI * 8208 + 16 + NWD * 4096 + 8192
    print("SBUF: RB", RB, "R_END", R_END, "limit", SB_END)
    rows_bc = A("rows_bc", [128, 4096], F32, at=RB)
    screp = A("screp", [128, NB, 8, 128], BF16, at=RB + 16384)
    actT = A("actT", [128, NF, 1024], BF16, at=RB)
    sg = A("sg", [128, 2, 512], F32, at=RB + 45056)
    assert RB + 45056 + 4096 <= R_END <= SB_END, (RB, R_END)
    stage = HB[:].bitcast(F32)

    pb = [nc.alloc_psum_tensor(f"pb{i}", [128, 512], F32) for i in range(8)]
    pbB = [Buf(f"pb{i}", excl=True) for i in range(8)]

    def slot(i):
        return wsl[:, i, :].rearrange("p (k n) -> p k n", k=8)

    xB = [Buf(f"x{t}") for t in range(8)]
    hB = [Buf(f"h{t}") for t in range(8)]
    kB = [Buf("k0"), Buf("k1")]
    vB = [Buf("v0"), Buf("v1")]
    wB = [Buf(f"ws{i}") for i in range(4)]
    qBf = Buf("q")
    uB = [Buf(f"u{g}") for g in range(4)]
    ptB = [Buf("pt1"), Buf("pt2")]
    poolB = [Buf("pooled0"), Buf("pooled1")]
    gIB = [Buf(f"gI{i}") for i in range(NGI)]
    wdB = [Buf(f"wd{i}") for i in range(NWD)]
    wTB = [Buf("wT0"), Buf("wT1")]
    constB = Buf("const")
    GB = Buf("G")
    ABB = Buf("AB")
    jBs = [Buf(f"junk{i}") for i in range(2)]
    jctr = [0]

    def next_junk():
        i = jctr[0] % 2
        jctr[0] += 1
        return junk[:, i, :], jBs[i]

    xnB = [Buf("xn0"), Buf("xn1")]
    tmpDB = xnB
    uhB = Buf("uhalo")
    actB = Buf("actT")
    sgB = [Buf("sg0"), Buf("sg1")]
    stgB = [Buf(f"stg{t}") for t in range(8)]
    setupB = Buf("setup")


    def fence(new, old):
        for nb_ in new:
            for ob_ in old:
                for dct in (ob_.w, ob_.r):
                    for k_, t_ in dct.items():
                        if k_ not in nb_.r or nb_.r[k_][1] < t_[1]:
                            nb_.r[k_] = t_

    dx = [sc.new_dsem(f"dx{t}") for t in range(8)]
    dw = [sc.new_dsem(f"dw{i}") for i in range(4)]
    dsetup = sc.new_dsem("dsetup")
    dsetup2 = sc.new_dsem("dsetup2")
    ddbgs = []

    def dbg(name, ap, shape, dt, reads):
        if name not in debug or name in dbg_outs:
            return
        t = nc.dram_tensor("dbg_" + name, list(shape), dt, kind="ExternalOutput").ap()
        dbg_outs[name] = t
        dd = sc.new_dsem("ddbg_" + name)
        ddbgs.append(dd)
        sc.dma("sp", t, ap, dd, reads=reads)

    evac_ctr = [0]

    def evac_copy(out, in_, reads, writes, nowaw=True, eng=None):
        e = eng
        if e is None:
            e = "act" if evac_ctr[0] % 2 == 0 else "dve"
            evac_ctr[0] += 1
        if e == "act":
            return sc.op("act", lambda en: en.activation(out=out, in_=in_, func=AF.Copy), reads=reads, writes=writes, nowaw=nowaw)
        return sc.op("dve", lambda en: en.tensor_copy(out=out, in_=in_), reads=reads, writes=writes, nowaw=nowaw)

    def evac_affine(out, in_, scale_ap, bias_ap, reads, writes, eng):
        if eng == "act":
            if bias_ap is None:
                return sc.op("act", lambda en: en.activation(out=out, in_=in_, func=AF.Identity, scale=scale_ap), reads=reads, writes=writes, nowaw=True)
            return sc.op("act", lambda en: en.activation(out=out, in_=in_, func=AF.Identity, scale=scale_ap, bias=bias_ap), reads=reads, writes=writes, nowaw=True)
        if bias_ap is None:
            return sc.op("dve", lambda en: en.tensor_scalar(out=out, in0=in_, scalar1=scale_ap, scalar2=None, op0=ALU.mult), reads=reads, writes=writes, nowaw=True)
        return sc.op("dve", lambda en: en.tensor_scalar(out=out, in0=in_, scalar1=scale_ap, scalar2=bias_ap, op0=ALU.mult, op1=ALU.add), reads=reads, writes=writes, nowaw=True)

    def mm_group(mms, reads, writes, nowaw=False):
        sc.begin("pe", reads, writes, nowaw)
        ins = None
        for (o, l, r, st, sp_) in mms:
            ins = nc.tensor.matmul(o, lhsT=l, rhs=r, start=st, stop=sp_)
        return sc.end("pe", ins, reads, writes)

    def load_w(si, dram_ap, ncols=512):
        sc.dma("pool", slot(si)[:, :, 0:ncols], dram_ap, dw[si], writes=[wB[si]])

    def wview(w_d, c0, c1):
        return w_d.rearrange("(k p) n -> p k n", p=128)[:, :, c0:c1]

    sc.op("pool", lambda e: e.memset(ident[:], 1.0), writes=[constB])
    sc.op("pool", lambda e: e.affine_select(out=ident[:], in_=ident[:], pattern=[[-1, 128]], compare_op=ALU.is_equal,
                                             fill=0.0, base=0, channel_multiplier=1), reads=[constB], writes=[constB])
    sc.op("pool", lambda e: e.memset(maskneg[:], NEG), writes=[constB])
    sc.op("pool", lambda e: e.affine_select(out=maskneg[:], in_=maskneg[:], pattern=[[-1, 128]], compare_op=ALU.is_ge,
                                             fill=0.0, base=0, channel_multiplier=1), reads=[constB], writes=[constB])
    sc.op("pool", lambda e: e.memset(onesb[:], 1.0), writes=[constB])
    sc.op("pool", lambda e: e.memset(uhalo[:], 0.0), writes=[uhB])
    sc.op("pool", lambda e: e.iota(corri[:, 0:15], pattern=[[-1, 15]], base=15, channel_multiplier=0), writes=[constB])
    sc.op("dve", lambda e: e.tensor_copy(out=corr[:, 0, 0:15], in_=corri[:, 0:15]), reads=[constB], writes=[constB])
    for g in range(3, -1, -1):
        win = float(2 << g)
        sc.op("dve", lambda e: e.tensor_scalar(out=corr[:, g, 0:15], in0=corr[:, 0, 0:15], scalar1=win, scalar2=None, op0=ALU.min),
              reads=[constB], writes=[constB])
        sc.op("dve", lambda e: e.reciprocal(out=corr[:, g, 0:15], in_=corr[:, g, 0:15]), reads=[constB], writes=[constB])
        sc.op("dve", lambda e: e.tensor_scalar(out=corr[:, g, 0:15], in0=corr[:, g, 0:15], scalar1=win, scalar2=None, op0=ALU.mult),
              reads=[constB], writes=[constB])

    def finish():
        for t in range(8):
            if dx[t].count:
                nc.sync.wait_ge(dx[t].sem, dx[t].count)
        for d_ in ddbgs + [dsetup, dsetup2]:
            if d_.count:
                nc.sync.wait_ge(d_.sem, d_.count)
        for e_ in ("act", "dve", "pool", "pe"):
            if sc.cnt[e_]:
                nc.sync.wait_ge(sc.sem[e_], sc.cnt[e_])
        return nc, sorted(dbg_outs)

    if stop_after == "s0":
        return finish()
    sc.dma("sp", cols[:], cols_d, dsetup, writes=[setupB])
    sc.dma("sp", rows_bc[:], rows_d[0:1, :].to_broadcast([128, 4096]), dsetup, writes=[setupB])
    sc.dma("pool", wpool[:], wpool_d.rearrange("g c d -> c g d"), dsetup2, writes=[setupB])
    cT = cols[:, 0:8 * NB]
    bcol = cols[:, 8 * NB:8 * NB + 48]
    gpre = [cols[:, 8 * NB + 48:8 * NB + 56], cols[:, 8 * NB + 56:8 * NB + 64]]
    pscale = cols[:, 8 * NB + 64:8 * NB + 68]

    if stop_after == "s1":
        return finish()
    sc.op("act", lambda e: e.activation(out=sc32[:], in_=cT, func=AF.Silu), reads=[setupB], writes=[constB])
    sc.op("dve", lambda e: e.tensor_copy(out=scT[:], in_=sc32[:]), reads=[constB], writes=[constB])
    for b in range(NB):
        for kc in range(8):
            sc.op("dve", lambda e: e.tensor_scalar(out=screp[:, b, kc, :], in0=onesb[:], scalar1=sc32[:, kc * NB + b:kc * NB + b + 1],
                                                    scalar2=None, op0=ALU.mult), reads=[constB], writes=[setupB], nowaw=True)

    if stop_after == "s2":
        return finish()
    colslots = {0: (0, 0), 1: (0, 4), 2: (1, 0), 3: (1, 4), 6: (2, 0), 7: (2, 4), 8: (3, 0), 9: (3, 4)}
    rowslots = {4: (G1, 0, 0), 5: (G1, 0, 1), 10: (G2, 1, 0), 11: (G2, 1, 1)}
    modps = pb[7]
    for j in range(12):
        si = j % 4
        load_w(si, wview(wcond_d, j * 512, (j + 1) * 512))
        if j in colslots:
            v, kc0 = colslots[j]
            for c4 in range(4):
                ci = (v * 8 + kc0 + c4) * NB
                mm_group([(modps[:, ci:ci + NB], slot(si)[:, kc, c4 * 128:(c4 + 1) * 128], scT[:, kc * NB:(kc + 1) * NB], kc == 0, kc == 7)
                          for kc in range(8)], reads=[wB[si], constB], writes=[pbB[7]], nowaw=True)
        else:
            Gt, gi_, half = rowslots[j]
            for b in range(NB):
                bank = (j + b) % 4
                mm_group([(pb[bank][:, :], screp[:, b, kc, :], slot(si)[:, kc, :], kc == 0, kc == 7) for kc in range(8)],
                         reads=[wB[si], setupB], writes=[pbB[bank]])
                sc.op("dve", lambda e: e.tensor_tensor(out=Gt[:, b, half * 512:(half + 1) * 512], in0=pb[bank][:, :],
                                                        in1=rows_bc[:, gi_ * 1024 + half * 512:gi_ * 1024 + (half + 1) * 512], op=ALU.add),
                      reads=[pbB[bank], setupB], writes=[GB], nowaw=True)
                sc.op("dve", lambda e: e.tensor_tensor(out=Gt[:, b, half * 512:(half + 1) * 512], in0=Gt[:, b, half * 512:(half + 1) * 512],
                                                        in1=rows_bc[:, 2048 + gi_ * 1024 + half * 512:2048 + gi_ * 1024 + (half + 1) * 512], op=ALU.mult),
                      reads=[GB, setupB], writes=[GB])
    if stop_after == "s3":
        return finish()
    bchunk = {0: 0, 1: 8, 2: 24, 3: 32}
    mp = modps[:, 0:32 * NB].rearrange("p (v k b) -> p v k b", v=4, k=8)
    for b in range(NB):
        for v in range(4):
            sc.op("dve", lambda e: e.tensor_tensor(out=AB[:, v, :, b], in0=mp[:, v, :, b], in1=bcol[:, bchunk[v]:bchunk[v] + 8], op=ALU.add),
                  reads=[pbB[7], setupB], writes=[ABB], nowaw=True)
        for v, gp in ((1, gpre[0]), (3, gpre[1])):
            sc.op("dve", lambda e: e.scalar_tensor_tensor(out=AB[:, v, :, b], in0=AB[:, v, :, b], scalar=1.0, in1=gp, op0=ALU.add, op1=ALU.mult),
                  reads=[ABB, setupB], writes=[ABB])
    dbg("AB", AB[:], [128, 4, 8, NB], F32, [ABB])
    dbg("G1", G1[:], [128, NB, 1024], F32, [GB])
    dbg("G2", G2[:], [128, NB, 1024], F32, [GB])
    fence([qBf] + uB + ptB + poolB, [setupB])

    bank_rr = [0]

    def next_bank(choices):
        b = choices[bank_rr[0] % len(choices)]
        bank_rr[0] += 1
        return b

    def rstd_chain(ssq_aps, tmp_ap, out_ap, rbufs, tbuf, obuf):
        sc.op("dve", lambda e: e.tensor_scalar(out=tmp_ap, in0=ssq_aps[0], scalar1=1.0 / D, scalar2=EPS, op0=ALU.mult, op1=ALU.add),
              reads=rbufs, writes=[tbuf])
        for extra in ssq_aps[1:]:
            sc.op("dve", lambda e: e.scalar_tensor_tensor(out=tmp_ap, in0=extra, scalar=1.0 / D, in1=tmp_ap, op0=ALU.mult, op1=ALU.add),
                  reads=rbufs + [tbuf], writes=[tbuf])
        sc.op("act", lambda e: e.activation(out=tmp_ap, in_=tmp_ap, func=AF.Sqrt), reads=[tbuf], writes=[tbuf])
        sc.op("dve", lambda e: e.reciprocal(out=out_ap, in_=tmp_ap), reads=[tbuf], writes=[obuf])

    def norm_transpose(t, b, rstd_ap, rbuf, vA, vB_, slot_i, all_act=False):
        xsl = slot_i % 2
        if all_act:
            sc.op("act", lambda e: e.activation(out=xn[:, xsl, :], in_=xs[:, t, :], func=AF.Copy, scale=rstd_ap),
                  reads=[xB[t], rbuf], writes=[xnB[xsl]])
        else:
            sc.op("dve", lambda e: e.tensor_scalar(out=xn[:, xsl, :], in0=xs[:, t, :], scalar1=rstd_ap, scalar2=None, op0=ALU.mult),
                  reads=[xB[t], rbuf], writes=[xnB[xsl]])
        bank = 3 + (slot_i % 2)
        psT = pb[bank][:].bitcast(BF16)
        sc.begin("pe", [xnB[xsl], constB], [pbB[bank]])
        ins = None
        for kc in range(8):
            ins = nc.tensor.transpose(psT[:, kc * 128:(kc + 1) * 128], xn[:, xsl, kc * 128:(kc + 1) * 128], ident[:])
        sc.end("pe", ins, [xnB[xsl], constB], [pbB[bank]])
        for kc in range(8):
            evac_affine(HB[:, kc, t * 128:(t + 1) * 128], psT[:, kc * 128:(kc + 1) * 128], AB[:, vA, kc, b:b + 1], AB[:, vB_, kc, b:b + 1],
                        reads=[pbB[bank], ABB], writes=[hB[t]], eng="act" if (all_act or slot_i % 2 == 0) else "dve")

    SSQA = stat[:, 0:8]
    TMPA = stat[:, 8:16]
    RSTDA = stat[:, 16:24]
    ssqABs = [Buf(f"ssqA{t}") for t in range(8)]
    tmpAB, rstdAB = Buf("tmpA"), Buf("rstdA")
    SSQD = stat[:, 24:40]
    TMPD = stat[:, 40:48]
    RSTDD = stat[:, 48:56]
    SSQ2 = stat[:, 56:64]
    TMP2 = stat[:, 64:72]
    RSTD2 = stat[:, 72:80]
    SSQF = stat[:, 80:96]
    TMPF = stat[:, 96:104]
    RSTDF = stat[:, 104:112]
    ssqDB = [[Buf(f"ssqD{t}_{c}") for c in range(2)] for t in range(8)]
    tmpDsB = [Buf(f"tmpDs{t}") for t in range(8)]
    rstdDB = [Buf(f"rstdD{t}") for t in range(8)]
    ssq2B = [Buf(f"ssq2{t}") for t in range(8)]
    tmp2B = [Buf(f"tmp2{t}") for t in range(8)]
    rstd2B = [Buf(f"rstd2{t}") for t in range(8)]
    ssqFB = [[Buf(f"ssqF{t}_{c}") for c in range(2)] for t in range(8)]
    tmpFB = [Buf(f"tmpF{t}") for t in range(8)]
    rstdFB = [Buf(f"rstdF{t}") for t in range(8)]

    chunks = [(0, 1), (0, 0), (1, 1), (1, 0)][:nchunks]
    if stop_after == "setup":
        chunks = []
    for ci, (s, half) in enumerate(chunks):
        b = s
        r0 = half * 1024
        gt0 = r0 // 128
        first = (ci == 0)

        fence(hB, stgB)
        fence([qBf] + uB + ptB + poolB, [actB] + sgB + tmpDB)
        for t in range(8):
            sc.dma("sp", xs[:, t, :], x_d[s, r0 + t * 128:r0 + (t + 1) * 128, :], dx[t], writes=[xB[t]])
        for si in range(4):
            load_w(si, wview(win_d, si * 512, (si + 1) * 512))
        if stop_after == "A0":
            continue
        for t in range(8):
            jk, jb_ = next_junk()
            sc.op("act", lambda e: e.activation(out=jk, in_=xs[:, t, :], func=AF.Square, accum_out=SSQA[:, t:t + 1]),
                  reads=[xB[t]], writes=[jb_, ssqABs[t]])
        if stop_after == "A1":
            continue
        rstd_chain([SSQA], TMPA, RSTDA, ssqABs, tmpAB, rstdAB)
        if stop_after == "A2":
            continue
        for t in range(8):
            norm_transpose(t, b, RSTDA[:, t:t + 1], rstdAB, 1, 0, t)
        if first:
            dbg("hT", HB[:], [128, 8, 1024], BF16, hB)

        if stop_after == "A":
            continue
        pbanks = [0, 1, 2, 5, 6]
        sc.op("pool", lambda e: e.memset(qA[64:128, :, :], 0.0), writes=[qBf])
        sc.op("pool", lambda e: e.memset(qB[0:64, :, :], 0.0), writes=[qBf], nowaw=True)
        for j in range(4):
            for tc in range(2):
                bank = next_bank(pbanks)
                mm_group([(pb[bank][:, :], slot(0)[:, kc, j * 128:(j + 1) * 128], HB[:, kc, tc * 512:(tc + 1) * 512], kc == 0, kc == 7)
                          for kc in range(8)], reads=[wB[0]] + hB[tc * 4:(tc + 1) * 4], writes=[pbB[bank]])
                qe = "act" if (j * 2 + tc) % 2 == 0 else "dve"
                evac_copy(qA[0:64, j, tc * 512:(tc + 1) * 512], pb[bank][0:64, :], reads=[pbB[bank]], writes=[qBf], eng=qe)
                evac_copy(qB[64:128, j, tc * 512:(tc + 1) * 512], pb[bank][64:128, :], reads=[pbB[bank]], writes=[qBf], eng=qe)
        for j in range(4):
            for tc in range(2):
                bank = next_bank(pbanks)
                mm_group([(pb[bank][:, :], slot(1)[:, kc, j * 128:(j + 1) * 128], HB[:, kc, tc * 512:(tc + 1) * 512], kc == 0, kc == 7)
                          for kc in range(8)], reads=[wB[1]] + hB[tc * 4:(tc + 1) * 4], writes=[pbB[bank]])
                evac_copy(kT[:, j, r0 + tc * 512:r0 + (tc + 1) * 512], pb[bank][:, :], reads=[pbB[bank]], writes=[kB[half]])
        for t in range(8):
            bank = next_bank(pbanks)
            mm_group([(pb[bank][:, :], HB[:, kc, t * 128:(t + 1) * 128], slot(2)[:, kc, :], kc == 0, kc == 7) for kc in range(8)],
                     reads=[wB[2], hB[t]], writes=[pbB[bank]])
            evac_copy(vv[:, gt0 + t, :], pb[bank][:, :], reads=[pbB[bank]], writes=[vB[half]])
        for g in range(4):
            for tc in range(2):
                bank = next_bank(pbanks)
                mm_group([(pb[bank][:, :], slot(3)[:, kc, g * 128:(g + 1) * 128], HB[:, kc, tc * 512:(tc + 1) * 512], kc == 0, kc == 7)
                          for kc in range(8)], reads=[wB[3]] + hB[tc * 4:(tc + 1) * 4], writes=[pbB[bank]])
                evac_copy(uT[:, g, tc * 512:(tc + 1) * 512], pb[bank][:, :], reads=[pbB[bank]], writes=[uB[g]])
        if first:
            dbg("qA", qA[:], [128, 4, 1024], BF16, [qBf])
            dbg("qB", qB[:], [128, 4, 1024], BF16, [qBf])
            dbg("kT", kT[:, :, 1024:2048], [128, 4, 1024], BF16, kB)
            dbg("vv", vv[:, 8:16, :], [128, 8, 512], BF16, vB)
            dbg("uT", uT[:, :, 0:1024], [128, 4, 1024], F32, uB)
        for chh in range(2):
            load_w(chh, wview(wout_d, chh * 512, (chh + 1) * 512))

        if stop_after == "B":
            continue
        for g in range(4):
            win = 2 << g
            if half == 1:
                sc.op("dve", lambda e: e.memset(uT[:, g, 1024:1040], 0.0), writes=[uB[g]], nowaw=True)
                sc.op("dve", lambda e: e.tensor_copy(out=uhalo[:, g, :], in_=uT[:, g, 0:16]), reads=[uB[g]], writes=[uhB], nowaw=True)
            else:
                sc.op("dve", lambda e: e.tensor_copy(out=uT[:, g, 1024:1040], in_=uhalo[:, g, :]), reads=[uhB], writes=[uB[g]], nowaw=True)
            src, srcB = uT[:, g, :], uB[g]
            n, sh, k = 1040, 1, 0
            while sh < win:
                n -= sh
                dst, dstB = (pt1, ptB[0]) if k % 2 == 0 else (pt2, ptB[1])
                sc.op("dve", lambda e: e.tensor_tensor(out=dst[:, 0:n], in0=src[:, 0:n], in1=src[:, sh:sh + n], op=ALU.add),
                      reads=[srcB], writes=[dstB])
                src, srcB = dst[:, :], dstB
                sh *= 2
                k += 1
            if half == 1:
                sc.op("dve", lambda e: e.tensor_tensor(out=src[:, 1009:1024], in0=src[:, 1009:1024], in1=corr[:, g, 0:15], op=ALU.mult),
                      reads=[srcB, constB], writes=[srcB])
            ps_ = g % 2
            sc.op("dve", lambda e: e.scalar_tensor_tensor(out=pooled[:, ps_, :], in0=src[:, 0:1024], scalar=1.0 / win, in1=uT[:, g, 0:1024],
                                                           op0=ALU.mult, op1=ALU.subtract), reads=[srcB, uB[g]], writes=[poolB[ps_]])
            for tc in range(2):
                bank = next_bank(pbanks)
                mm_group([(pb[bank][:, :], wpool[:, g, :], pooled[:, ps_, tc * 512:(tc + 1) * 512], True, True)],
                         reads=[setupB, poolB[ps_]], writes=[pbB[bank]])
                evac_affine(HB[:, 4 + g, tc * 512:(tc + 1) * 512], pb[bank][:, :], pscale[:, g:g + 1], None,
                            reads=[pbB[bank], setupB], writes=hB[tc * 4:(tc + 1) * 4], eng="act" if tc == 0 else "dve")

        if stop_after == "P":
            continue
        items = [(qb, h) for qb in range(8) for h in range(8)]
        zbanks = [0, 1]
        zc = [0]

        def att_front(n):
            qb, h = items[n]
            gi = gt0 + qb
            nkb = 16 - gi
            L = nkb * 128
            sl = n % NGI
            qsrc = qA if h % 2 == 0 else qB
            j = h // 2
            nch = (L + 511) // 512
            for c in range(nch):
                ncl = min(512, L - c * 512)
                bank = zbanks[zc[0] % 2]
                zc[0] += 1
                mms = [(pb[bank][:, 0:ncl], qsrc[:, j, qb * 128:(qb + 1) * 128], kT[:, j, gi * 128 + c * 512:gi * 128 + c * 512 + ncl], True, c != 0)]
                if c == 0:
                    mms.append((pb[bank][:, 0:128], ident[:], maskneg[:], False, True))
                mm_group(mms, reads=[qBf, kB[0], kB[1], constB], writes=[pbB[bank]])
                sc.op("act", lambda e: e.activation(out=gI[:, sl, 1 + c * 512:1 + c * 512 + ncl], in_=pb[bank][:, 0:ncl], func=AF.Sigmoid, scale=-0.125),
                      reads=[pbB[bank]], writes=[gIB[sl]], nowaw=(c != 0))
            sc.op("dve", lambda e: e.tensor_tensor_scan(out=gI[:, sl, 1:L + 1], data0=gI[:, sl, 1:L + 1], data1=gI[:, sl, 1:L + 1],
                                                         initial=1.0, op0=ALU.mult, op1=ALU.min), reads=[gIB[sl]], writes=[gIB[sl]])
            sw = n % NWD
            Lp = (int(L * POOL_FRAC) // 2) * 2
            if Lp > 0:
                sc.op("pool", lambda e: e.tensor_tensor(out=wdf[:, sw, 0:Lp], in0=gI[:, sl, 0:Lp], in1=gI[:, sl, 1:Lp + 1], op=ALU.subtract),
                      reads=[gIB[sl]], writes=[wdB[sw]])
            sc.op("dve", lambda e: e.tensor_tensor(out=wdf[:, sw, Lp:L], in0=gI[:, sl, Lp:L], in1=gI[:, sl, Lp + 1:L + 1], op=ALU.subtract),
                  reads=[gIB[sl]], writes=[wdB[sw]], nowaw=(Lp > 0))

        def att_back(n):
            qb, h = items[n]
            gi = gt0 + qb
            nkb = 16 - gi
            sw = n % NWD
            st = n % 2
            j = h // 2
            kb = 0
            grp = 0
            while kb < nkb:
                ng = min(8, nkb - kb)
                bank = 3 + (grp + n) % 2
                psT = pb[bank][:].bitcast(BF16)
                sc.begin("pe", [wdB[sw], constB], [pbB[bank]])
                ins = None
                for i in range(ng):
                    ins = nc.tensor.transpose(psT[:, i * 128:(i + 1) * 128], wdf[:, sw, (kb + i) * 128:(kb + i + 1) * 128], ident[:])
                sc.end("pe", ins, [wdB[sw], constB], [pbB[bank]])
                sc.op("act", lambda e: e.activation(out=wT[:, st, kb * 128:(kb + ng) * 128], in_=psT[:, 0:ng * 128], func=AF.Copy),
                      reads=[pbB[bank]], writes=[wTB[st]], nowaw=(kb != 0))
                kb += ng
                grp += 1
            bank = 5 + (n % 2)
            mm_group([(pb[bank][:, 0:128], vv[:, gi + k2, j * 128:(j + 1) * 128], wT[:, st, k2 * 128:(k2 + 1) * 128], k2 == 0, k2 == nkb - 1)
                      for k2 in range(nkb)], reads=[vB[0], vB[1], wTB[st]], writes=[pbB[bank]])
            ph = (h % 2) * 64
            evac_copy(HB[ph:ph + 64, j, qb * 128:(qb + 1) * 128], pb[bank][ph:ph + 64, 0:128], reads=[pbB[bank]], writes=[hB[qb]],
                      eng="dve" if h % 2 == 0 else "act")

        def d_stages(t):
            banks = (2, 7)
            st = []

            def s_mm():
                for chh in range(2):
                    mm_group([(pb[banks[chh]][:, :], HB[:, kc, t * 128:(t + 1) * 128], slot(chh)[:, kc, :], kc == 0, kc == 7) for kc in range(8)],
                             reads=[wB[chh], hB[t]], writes=[pbB[banks[chh]]])
            st.append(s_mm)

            def s_sq():
                for chh in range(2):
                    jk, jb_ = next_junk()
                    sc.op("act", lambda e: e.activation(out=jk[:, 0:512], in_=pb[banks[chh]][:, :], func=AF.Square,
                                                         accum_out=SSQD[:, 2 * t + chh:2 * t + chh + 1]),
                          reads=[pbB[banks[chh]]], writes=[jb_, ssqDB[t][chh]])
            st.append(s_sq)

            def chain(ssq_aps, tmp_ap, out_ap, rbufs, tbuf, obuf, tail):
                def c1():
                    sc.op("dve", lambda e: e.tensor_scalar(out=tmp_ap, in0=ssq_aps[0], scalar1=1.0 / D, scalar2=EPS, op0=ALU.mult, op1=ALU.add),
                          reads=rbufs, writes=[tbuf])
                    for extra in ssq_aps[1:]:
                        sc.op("dve", lambda e: e.scalar_tensor_tensor(out=tmp_ap, in0=extra, scalar=1.0 / D, in1=tmp_ap, op0=ALU.mult, op1=ALU.add),
                              reads=rbufs + [tbuf], writes=[tbuf])

                def c2():
                    sc.op("act", lambda e: e.activation(out=tmp_ap, in_=tmp_ap, func=AF.Sqrt), reads=[tbuf], writes=[tbuf])

                def c3():
                    sc.op("dve", lambda e: e.reciprocal(out=out_ap, in_=tmp_ap), reads=[tbuf], writes=[obuf])
                    tail()
                return [c1, c2, c3]

            def stt_tail():
                for chh in range(2):
                    sc.op("dve", lambda e: e.scalar_tensor_tensor(out=tmpD[:, chh, :], in0=pb[banks[chh]][:, :], scalar=RSTDD[:, t:t + 1],
                                                                   in1=G1[:, b, chh * 512:(chh + 1) * 512], op0=ALU.mult, op1=ALU.mult),
                          reads=[pbB[banks[chh]], rstdDB[t], GB], writes=[tmpDB[chh]])
            st.extend(chain([SSQD[:, 2 * t:2 * t + 1], SSQD[:, 2 * t + 1:2 * t + 2]], TMPD[:, t:t + 1], RSTDD[:, t:t + 1],
                            ssqDB[t], tmpDsB[t], rstdDB[t], stt_tail))

            def s_add():
                for chh in range(2):
                    sc.op("pool", lambda e: e.tensor_tensor(out=xs[:, t, chh * 512:(chh + 1) * 512], in0=xs[:, t, chh * 512:(chh + 1) * 512],
                                                             in1=tmpD[:, chh, :], op=ALU.add), reads=[tmpDB[chh], xB[t]], writes=[xB[t]])
            st.append(s_add)

            def s_sq2():
                jk, jb_ = next_junk()
                sc.op("act", lambda e: e.activation(out=jk, in_=xs[:, t, :], func=AF.Square, accum_out=SSQ2[:, t:t + 1]),
                      reads=[xB[t]], writes=[jb_, ssq2B[t]])
            st.append(s_sq2)
            st.extend(chain([SSQ2[:, t:t + 1]], TMP2[:, t:t + 1], RSTD2[:, t:t + 1], [ssq2B[t]], tmp2B[t], rstd2B[t], lambda: None))
            xsl = t % 2
            bank = 3 + (t % 2)
            psT = pb[bank][:].bitcast(BF16)

            def s_xn():
                sc.op("act", lambda e: e.activation(out=xn[:, xsl, :], in_=xs[:, t, :], func=AF.Copy, scale=RSTD2[:, t:t + 1]),
                      reads=[xB[t], rstd2B[t]], writes=[xnB[xsl]])
            st.append(s_xn)

            def s_tr():
                sc.begin("pe", [xnB[xsl], constB], [pbB[bank]])
                ins = None
                for kc in range(8):
                    ins = nc.tensor.transpose(psT[:, kc * 128:(kc + 1) * 128], xn[:, xsl, kc * 128:(kc + 1) * 128], ident[:])
                sc.end("pe", ins, [xnB[xsl], constB], [pbB[bank]])
                for kc in range(8):
                    evac_affine(HB[:, kc, t * 128:(t + 1) * 128], psT[:, kc * 128:(kc + 1) * 128], AB[:, 3, kc, b:b + 1], AB[:, 2, kc, b:b + 1],
                                reads=[pbB[bank], ABB], writes=[hB[t]], eng="act")
            st.append(s_tr)
            return st

        dq = []

        def d_pump():
            for lst in dq:
                if lst:
                    lst.pop(0)()

        fence(gIB + wdB + wTB, uB + ptB + poolB)
        for sl in range(NGI):
            sc.op("dve", lambda e: e.memset(gI[:, sl, 0:1], 1.0), reads=[], writes=[gIB[sl]])
        NI = len(items)
        for n in range(NI + LA):
            if n < NI:
                att_front(n)
            if n >= LA:
                d_pump()
                att_back(n - LA)
                if (n - LA) % 8 == 7:
                    dq.append(d_stages((n - LA) // 8))
        while any(dq):
            d_pump()

        if stop_after == "C":
            continue
        load_w(2, wview(wgate_d, 0, 512))
        load_w(3, wview(wup_d, 0, 512))

        if first:
            dbg("x1", xs[:], [128, 8, 1024], F32, xB)
            dbg("h2T", HB[:], [128, 8, 1024], BF16, hB)

        if stop_after == "D":
            continue
        fence([actB] + sgB, [qBf] + gIB + wdB + wTB + tmpDB + uB + ptB + poolB)
        nfs = 6
        gu_banks = [(0, 1), (2, 5)]
        it = 0
        for fs in range(nfs):
            sg_, su_ = (2, 3) if fs % 2 == 0 else (0, 1)
            nfl = min(4, NF - fs * 4)
            if fs + 1 < nfs:
                ng_, nu_ = (0, 1) if fs % 2 == 0 else (2, 3)
                c0 = (fs + 1) * 512
                c1 = min(DFF, c0 + 512)
                load_w(ng_, wview(wgate_d, c0, c1), c1 - c0)
                load_w(nu_, wview(wup_d, c0, c1), c1 - c0)
            for f4 in range(nfl):
                f = fs * 4 + f4
                for tc in range(2):
                    gb_, ub_ = gu_banks[it % 2]
                    sgs = it % 2
                    it += 1
                    mm_group([(pb[gb_][:, :], slot(sg_)[:, kc, f4 * 128:(f4 + 1) * 128], HB[:, kc, tc * 512:(tc + 1) * 512], kc == 0, kc == 7)
                              for kc in range(8)], reads=[wB[sg_]] + hB[tc * 4:(tc + 1) * 4], writes=[pbB[gb_]])
                    mm_group([(pb[ub_][:, :], slot(su_)[:, kc, f4 * 128:(f4 + 1) * 128], HB[:, kc, tc * 512:(tc + 1) * 512], kc == 0, kc == 7)
                              for kc in range(8)], reads=[wB[su_]] + hB[tc * 4:(tc + 1) * 4], writes=[pbB[ub_]])
                    sc.op("act", lambda e: e.activation(out=sg[:, sgs, :], in_=pb[gb_][:, :], func=AF.Silu), reads=[pbB[gb_]], writes=[sgB[sgs]])
                    sc.op("dve", lambda e: e.tensor_tensor(out=actT[:, f, tc * 512:(tc + 1) * 512], in0=sg[:, sgs, :], in1=pb[ub_][:, :], op=ALU.mult),
                          reads=[sgB[sgs], pbB[ub_]], writes=[actB], nowaw=True)
        if first:
            dbg("actT", actT[:], [128, NF, 1024], BF16, [actB])

        if stop_after == "E":
            continue
        wdv = wdown_d.rearrange("(f p) n -> p f n", p=128)
        dslots = [(0, 8), (8, 8), (16, 6)]
        useq = 0
        for chh in range(2):
            for di, (f0, nfd) in enumerate(dslots):
                si = useq % 4
                useq += 1
                sc.dma("pool", slot(si)[:, 0:nfd, :], wdv[:, f0:f0 + nfd, chh * 512:(chh + 1) * 512], dw[si], writes=[wB[si]])
                for f8 in range(nfd):
                    f = f0 + f8
                    for t in range(8):
                        mm_group([(pb[t][:, :], actT[:, f, t * 128:(t + 1) * 128], slot(si)[:, f8, :], f == 0, f == NF - 1)],
                                 reads=[wB[si], actB], writes=[pbB[t]])
            for t in range(8):
                jk, jb_ = next_junk()
                sc.op("act", lambda e: e.activation(out=jk[:, 0:512], in_=pb[t][:, :], func=AF.Square,
                                                     accum_out=SSQF[:, 2 * t + chh:2 * t + chh + 1]),
                      reads=[pbB[t]], writes=[jb_, ssqFB[t][chh]])
                if chh == 0:
                    sc.op("dve", lambda e: e.tensor_copy(out=stage[:, t, :], in_=pb[t][:, :]),
                          reads=[pbB[t]], writes=[stgB[t]] + hB)
            if chh == 1:
                sq2 = SSQF.rearrange("p (t c) -> p t c", c=2)
                rstd_chain([sq2[:, :, 0], sq2[:, :, 1]], TMPF, RSTDF, [x_ for p_ in ssqFB for x_ in p_], tmpFB[0], rstdFB[0])
                for t in range(8):
                    for c2 in range(2):
                        src_ap = stage[:, t, :] if c2 == 0 else pb[t][:, :]
                        srcbuf = stgB[t] if c2 == 0 else pbB[t]
                        ts_ = c2
                        sc.op("dve", lambda e: e.scalar_tensor_tensor(out=tmpD[:, ts_, :], in0=src_ap, scalar=RSTDF[:, t:t + 1],
                                                                       in1=G2[:, b, c2 * 512:(c2 + 1) * 512], op0=ALU.mult, op1=ALU.mult),
                              reads=[srcbuf, rstdFB[0], GB], writes=[tmpDB[ts_]])
                        sc.op("pool", lambda e: e.tensor_tensor(out=xs[:, t, c2 * 512:(c2 + 1) * 512], in0=xs[:, t, c2 * 512:(c2 + 1) * 512],
                                                                 in1=tmpD[:, ts_, :], op=ALU.add), reads=[tmpDB[ts_], xB[t]], writes=[xB[t]])
                    sc.dma("sp", out_d[s, r0 + t * 128:r0 + (t + 1) * 128, :], xs[:, t, :], dx[t], reads=[xB[t]])

    return finish()


def make_in_maps(x, c, w_cond, b_cond, g_mix_pre, g_mix_post, w_in, w_pool, pool_scale, w_out,
                 g_ffn_pre, g_ffn_post, w_gate, w_up, w_down):
    f = lambda a: np.ascontiguousarray(np.asarray(a, dtype=np.float32))
    x = f(x)
    c = f(c)
    xr = x[:, ::-1, :]
    bc = f(b_cond)[0]
    colv = lambda v: v.reshape(-1, 128).T
    rows = np.concatenate([bc[2048:3072], bc[5120:6144], f(g_mix_post)[0], f(g_ffn_post)[0]])[None, :]
    shared = dict(rows=f(rows), w_cond=f(w_cond)[0], w_in=f(w_in)[0], w_pool=f(w_pool)[0], w_out=f(w_out)[0],
                  w_gate=f(w_gate)[0], w_up=f(w_up)[0], w_down=f(w_down)[0])
    maps = []
    for i in range(NCORES):
        cb = c[i * NB:(i + 1) * NB]
        cT = cb.reshape(NB, 8, 128).transpose(2, 1, 0).reshape(128, 8 * NB)
        cols = np.concatenate([cT, colv(bc), colv(f(g_mix_pre)[0]), colv(f(g_ffn_pre)[0]), colv(f(pool_scale)[0])], axis=1)
        m = dict(shared)
        m["x"] = f(xr[i * NB:(i + 1) * NB])
        m["cols"] = f(cols)
        maps.append(m)
    return maps


_NC_CACHE = {}


def kernel(**inputs):
    if "nc" not in _NC_CACHE:
        _NC_CACHE["nc"] = build_nc()[0]
    nc = _NC_CACHE["nc"]
    maps = make_in_maps(**inputs)
    res = run_bass_kernel_spmd(nc, maps, core_ids=list(range(NCORES)))
    outs = [np.asarray(r["out"]) for r in res.results]
    full = np.concatenate(outs, axis=0)[:, ::-1, :]
    return np.ascontiguousarray(full.astype(np.float32))
```

```python
import numpy as np
from contextlib import ExitStack
import concourse.bass as bass
import concourse.mybir as mybir
from concourse.bass_utils import run_bass_kernel_spmd

F32 = mybir.dt.float32
BF16 = mybir.dt.bfloat16
I32 = mybir.dt.int32
AF = mybir.ActivationFunctionType
ALU = mybir.AluOpType

D = 1024
S = 2048
NB = 2
NCORES = 8
DFF = 2816
NF = 22
EPS = 1e-6
NCOLS = 8 * NB + 48 + 8 + 8 + 4
SB_BASE = 16512
SB_END = 229376
NEG = -30000.0
POOL_FRAC = 0.0


class Buf:
    __slots__ = ("name", "w", "r", "excl")

    def __init__(self, name, excl=False):
        self.name = name
        self.w = {}
        self.r = {}
        self.excl = excl


class DSem:
    def __init__(self, sem, key):
        self.sem = sem
        self.count = 0
        self.key = key


class Sched:
    def __init__(self, nc, es):
        self.nc = nc
        self.es = es
        self.eng = dict(pe=nc.tensor, act=nc.scalar, dve=nc.vector, pool=nc.gpsimd, sp=nc.sync)
        self.sem = {k: es.enter_context(nc.semaphore("sem_" + k)) for k in self.eng}
        self.cnt = {k: 0 for k in self.eng}
        self.known = {k: {} for k in self.eng}
        self.nwaits = 0
        self.nds = 0

    def new_dsem(self, name):
        self.nds += 1
        return DSem(self.es.enter_context(self.nc.semaphore(name)), name)

    def _wait(self, e, t):
        sem, val, key = t
        if key == e and e == "pe":
            return
        if self.known[e].get(key, 0) >= val:
            return
        self.eng[e].wait_ge(sem, val)
        self.known[e][key] = val
        self.nwaits += 1

    def begin(self, e, reads=(), writes=(), nowaw=False):
        for b in reads:
            for t in b.w.values():
                self._wait(e, t)
            if b.excl:
                for k, t in b.r.items():
                    if k != e:
                        self._wait(e, t)
        for b in writes:
            if not nowaw:
                for t in b.w.values():
                    self._wait(e, t)
            for t in b.r.values():
                self._wait(e, t)

    def end(self, e, ins, reads=(), writes=()):
        self.cnt[e] += 1
        ins.then_inc(self.sem[e], 1)
        t = (self.sem[e], self.cnt[e], e)
        for b in reads:
            b.r[e] = t
        for b in writes:
            b.w[e] = t
        return t

    def op(self, e, fn, reads=(), writes=(), nowaw=False):
        self.begin(e, reads, writes, nowaw)
        ins = fn(self.eng[e])
        return self.end(e, ins, reads, writes)

    def dma(self, q, out, in_, dsem, reads=(), writes=(), nowaw=False):
        self.begin(q, reads, writes, nowaw)
        ins = self.eng[q].dma_start(out=out, in_=in_)
        dsem.count += 16
        ins.then_inc(dsem.sem, 16)
        t = (dsem.sem, dsem.count, dsem.key)
        for b in reads:
            b.r[dsem.key] = t
        for b in writes:
            b.w[dsem.key] = t
        return t


def build_nc(debug=(), nchunks=4, stop_after=None):
    nc = bass.Bass("TRN2", target_bir_lowering=False)
    es = ExitStack()
    sc = Sched(nc, es)
    dbg_outs = {}

    def din(name, shape):
        return nc.dram_tensor(name, list(shape), F32, kind="ExternalInput").ap()

    x_d = din("x", [NB, S, D])
    cols_d = din("cols", [128, NCOLS])
    rows_d = din("rows", [1, 4096])
    wcond_d = din("w_cond", [D, 6 * D])
    win_d = din("w_in", [D, 2048])
    wpool_d = din("w_pool", [4, 128, 128])
    wout_d = din("w_out", [D, D])
    wgate_d = din("w_gate", [D, DFF])
    wup_d = din("w_up", [D, DFF])
    wdown_d = din("w_down", [DFF, D])
    out_d = nc.dram_tensor("out", [NB, S, D], F32, kind="ExternalOutput").ap()

    off = [SB_BASE]
    ISZ = {F32: 4, BF16: 2, I32: 4}

    def A(name, shape, dt, at=None):
        n = ISZ[dt]
        for d_ in shape[1:]:
            n *= d_
        o = off[0] if at is None else at
        o = (o + 63) // 64 * 64
        t = nc.alloc_sbuf_tensor_at(name, list(shape), dt, offset=o)
        if at is None:
            off[0] = o + n
        assert o + n <= SB_END, (name, o, n)
        return t

    xs = A("xs", [128, 8, 1024], F32)
    HB = A("HB", [128, 8, 1024], BF16)
    kT = A("kT", [128, 4, 2048], BF16)
    vv = A("vv", [128, 16, 512], BF16)
    wsl = A("wsl", [128, 4, 4096], BF16)
    G1 = A("G1", [128, NB, 1024], F32)
    G2 = A("G2", [128, NB, 1024], F32)
    ident = A("ident", [128, 128], BF16)
    maskneg = A("maskneg", [128, 128], BF16)
    onesb = A("onesb", [128, 128], BF16)
    wpool = A("wpool", [128, 4, 128], BF16)
    cols = A("cols_sb", [128, NCOLS], F32)
    AB = A("AB", [128, 4, 8, NB], F32)
    stat = A("stat", [128, 128], F32)
    corr = A("corr", [128, 4, 16], F32)
    corri = A("corri", [128, 16], I32)
    uhalo = A("uhalo", [128, 4, 16], F32)
    sc32 = A("sc32", [128, 8 * NB], F32)
    scT = A("scT", [128, 8 * NB], BF16)
    xn = A("xn", [128, 2, 1024], BF16)
    junk = A("junk", [128, 2, 1024], BF16)
    RB = (off[0] + 63) // 64 * 64
    qA = A("qA", [128, 4, 1024], BF16, at=RB)
    qB = A("qB", [128, 4, 1024], BF16, at=RB + 8192)
    R2 = RB + 16384
    uT = A("uT", [128, 4, 1040], F32, at=R2)
    pt1 = A("pt1", [128, 1040], F32, at=R2 + 16640)
    pt2 = A("pt2", [128, 1040], F32, at=R2 + 16640 + 4160)
    pooled = A("pooled", [128, 2, 1024], BF16, at=R2 + 16640 + 8320)
    NGI, NWD, LA = 4, 3, 2
    gI = A("gI", [128, NGI, 2052], F32, at=R2)
    wdf = A("wdf", [128, NWD, 2048], BF16, at=R2 + NGI * 8208 + 16)
    wT = A("wT", [128, 2, 2048], BF16, at=R2 + NGI * 8208 + 16 + NWD * 4096)
    tmpD = A("tmpD", [128, 2, 512], F32, at=RB + 45056)
    R_END = R2 + NGI * 8208 + 16 + NWD * 4096 + 8192
    print("SBUF: RB", RB, "R_END", R_END, "limit", SB_END)
    rows_bc = A("rows_bc", [128, 4096], F32, at=RB)
    screp = A("screp", [128, NB, 8, 128], BF16, at=RB + 16384)
    actT = A("actT", [128, NF, 1024], BF16, at=RB)
    sg = A("sg", [128, 2, 512], F32, at=RB + 45056)
    assert RB + 45056 + 4096 <= R_END <= SB_END, (RB, R_END)
    stage = HB[:].bitcast(F32)

    pb = [nc.alloc_psum_tensor(f"pb{i}", [128, 512], F32) for i in range(8)]
    pbB = [Buf(f"pb{i}", excl=True) for i in range(8)]

    def slot(i):
        return wsl[:, i, :].rearrange("p (k n) -> p k n", k=8)

    xB = [Buf(f"x{t}") for t in range(8)]
    hB = [Buf(f"h{t}") for t in range(8)]
    kB = [Buf("k0"), Buf("k1")]
    vB = [Buf("v0"), Buf("v1")]
    wB = [Buf(f"ws{i}") for i in range(4)]
    qBf = Buf("q")
    uB = [Buf(f"u{g}") for g in range(4)]
    ptB = [Buf("pt1"), Buf("pt2")]
    poolB = [Buf("pooled0"), Buf("pooled1")]
    gIB = [Buf(f"gI{i}") for i in range(NGI)]
    wdB = [Buf(f"wd{i}") for i in range(NWD)]
    wTB = [Buf("wT0"), Buf("wT1")]
    constB = Buf("const")
    GB = Buf("G")
    ABB = Buf("AB")
    jBs = [Buf(f"junk{i}") for i in range(2)]
    jctr = [0]

    def next_junk():
        i = jctr[0] % 2
        jctr[0] += 1
        return junk[:, i, :], jBs[i]

    xnB = [Buf("xn0"), Buf("xn1")]
    uhB = Buf("uhalo")
    actB = Buf("actT")
    sgB = [Buf("sg0"), Buf("sg1")]
    tmpDB = [Buf("tmpD0"), Buf("tmpD1")]
    stgB = [Buf(f"stg{t}") for t in range(8)]
    setupB = Buf("setup")


    def fence(new, old):
        for nb_ in new:
            for ob_ in old:
                for dct in (ob_.w, ob_.r):
                    for k_, t_ in dct.items():
                        if k_ not in nb_.r or nb_.r[k_][1] < t_[1]:
                            nb_.r[k_] = t_

    dx = [sc.new_dsem(f"dx{t}") for t in range(8)]
    dw = [sc.new_dsem(f"dw{i}") for i in range(4)]
    dsetup = sc.new_dsem("dsetup")
    dsetup2 = sc.new_dsem("dsetup2")
    ddbgs = []

    def dbg(name, ap, shape, dt, reads):
        if name not in debug or name in dbg_outs:
            return
        t = nc.dram_tensor("dbg_" + name, list(shape), dt, kind="ExternalOutput").ap()
        dbg_outs[name] = t
        dd = sc.new_dsem("ddbg_" + name)
        ddbgs.append(dd)
        sc.dma("sp", t, ap, dd, reads=reads)

    evac_ctr = [0]

    def evac_copy(out, in_, reads, writes, nowaw=True, eng=None):
        e = eng
        if e is None:
            e = "act" if evac_ctr[0] % 2 == 0 else "dve"
            evac_ctr[0] += 1
        if e == "act":
            return sc.op("act", lambda en: en.activation(out=out, in_=in_, func=AF.Copy), reads=reads, writes=writes, nowaw=nowaw)
        return sc.op("dve", lambda en: en.tensor_copy(out=out, in_=in_), reads=reads, writes=writes, nowaw=nowaw)

    def evac_affine(out, in_, scale_ap, bias_ap, reads, writes, eng):
        if eng == "act":
            if bias_ap is None:
                return sc.op("act", lambda en: en.activation(out=out, in_=in_, func=AF.Identity, scale=scale_ap), reads=reads, writes=writes, nowaw=True)
            return sc.op("act", lambda en: en.activation(out=out, in_=in_, func=AF.Identity, scale=scale_ap, bias=bias_ap), reads=reads, writes=writes, nowaw=True)
        if bias_ap is None:
            return sc.op("dve", lambda en: en.tensor_scalar(out=out, in0=in_, scalar1=scale_ap, scalar2=None, op0=ALU.mult), reads=reads, writes=writes, nowaw=True)
        return sc.op("dve", lambda en: en.tensor_scalar(out=out, in0=in_, scalar1=scale_ap, scalar2=bias_ap, op0=ALU.mult, op1=ALU.add), reads=reads, writes=writes, nowaw=True)

    def mm_group(mms, reads, writes, nowaw=False):
        sc.begin("pe", reads, writes, nowaw)
        ins = None
        for (o, l, r, st, sp_) in mms:
            ins = nc.tensor.matmul(o, lhsT=l, rhs=r, start=st, stop=sp_)
        return sc.end("pe", ins, reads, writes)

    def load_w(si, dram_ap, ncols=512):
        sc.dma("pool", slot(si)[:, :, 0:ncols], dram_ap, dw[si], writes=[wB[si]])

    def wview(w_d, c0, c1):
        return w_d.rearrange("(k p) n -> p k n", p=128)[:, :, c0:c1]

    sc.op("pool", lambda e: e.memset(ident[:], 1.0), writes=[constB])
    sc.op("pool", lambda e: e.affine_select(out=ident[:], in_=ident[:], pattern=[[-1, 128]], compare_op=ALU.is_equal,
                                             fill=0.0, base=0, channel_multiplier=1), reads=[constB], writes=[constB])
    sc.op("pool", lambda e: e.memset(maskneg[:], NEG), writes=[constB])
    sc.op("pool", lambda e: e.affine_select(out=maskneg[:], in_=maskneg[:], pattern=[[-1, 128]], compare_op=ALU.is_ge,
                                             fill=0.0, base=0, channel_multiplier=1), reads=[constB], writes=[constB])
    sc.op("pool", lambda e: e.memset(onesb[:], 1.0), writes=[constB])
    sc.op("pool", lambda e: e.memset(uhalo[:], 0.0), writes=[uhB])
    sc.op("pool", lambda e: e.iota(corri[:, 0:15], pattern=[[-1, 15]], base=15, channel_multiplier=0), writes=[constB])
    sc.op("dve", lambda e: e.tensor_copy(out=corr[:, 0, 0:15], in_=corri[:, 0:15]), reads=[constB], writes=[constB])
    for g in range(3, -1, -1):
        win = float(2 << g)
        sc.op("dve", lambda e: e.tensor_scalar(out=corr[:, g, 0:15], in0=corr[:, 0, 0:15], scalar1=win, scalar2=None, op0=ALU.min),
              reads=[constB], writes=[constB])
        sc.op("dve", lambda e: e.reciprocal(out=corr[:, g, 0:15], in_=corr[:, g, 0:15]), reads=[constB], writes=[constB])
        sc.op("dve", lambda e: e.tensor_scalar(out=corr[:, g, 0:15], in0=corr[:, g, 0:15], scalar1=win, scalar2=None, op0=ALU.mult),
              reads=[constB], writes=[constB])

    def finish():
        for t in range(8):
            if dx[t].count:
                nc.sync.wait_ge(dx[t].sem, dx[t].count)
        for d_ in ddbgs + [dsetup, dsetup2]:
            if d_.count:
                nc.sync.wait_ge(d_.sem, d_.count)
        for e_ in ("act", "dve", "pool", "pe"):
            if sc.cnt[e_]:
                nc.sync.wait_ge(sc.sem[e_], sc.cnt[e_])
        return nc, sorted(dbg_outs)

    if stop_after == "s0":
        return finish()
    sc.dma("sp", cols[:], cols_d, dsetup, writes=[setupB])
    sc.dma("sp", rows_bc[:], rows_d[0:1, :].to_broadcast([128, 4096]), dsetup, writes=[setupB])
    sc.dma("pool", wpool[:], wpool_d.rearrange("g c d -> c g d"), dsetup2, writes=[setupB])
    cT = cols[:, 0:8 * NB]
    bcol = cols[:, 8 * NB:8 * NB + 48]
    gpre = [cols[:, 8 * NB + 48:8 * NB + 56], cols[:, 8 * NB + 56:8 * NB + 64]]
    pscale = cols[:, 8 * NB + 64:8 * NB + 68]

    if stop_after == "s1":
        return finish()
    sc.op("act", lambda e: e.activation(out=sc32[:], in_=cT, func=AF.Silu), reads=[setupB], writes=[constB])
    sc.op("dve", lambda e: e.tensor_copy(out=scT[:], in_=sc32[:]), reads=[constB], writes=[constB])
    for b in range(NB):
        for kc in range(8):
            sc.op("dve", lambda e: e.tensor_scalar(out=screp[:, b, kc, :], in0=onesb[:], scalar1=sc32[:, kc * NB + b:kc * NB + b + 1],
                                                    scalar2=None, op0=ALU.mult), reads=[constB], writes=[setupB], nowaw=True)

    if stop_after == "s2":
        return finish()
    colslots = {0: (0, 0), 1: (0, 4), 2: (1, 0), 3: (1, 4), 6: (2, 0), 7: (2, 4), 8: (3, 0), 9: (3, 4)}
    rowslots = {4: (G1, 0, 0), 5: (G1, 0, 1), 10: (G2, 1, 0), 11: (G2, 1, 1)}
    modps = pb[7]
    for j in range(12):
        si = j % 4
        load_w(si, wview(wcond_d, j * 512, (j + 1) * 512))
        if j in colslots:
            v, kc0 = colslots[j]
            for c4 in range(4):
                ci = (v * 8 + kc0 + c4) * NB
                mm_group([(modps[:, ci:ci + NB], slot(si)[:, kc, c4 * 128:(c4 + 1) * 128], scT[:, kc * NB:(kc + 1) * NB], kc == 0, kc == 7)
                          for kc in range(8)], reads=[wB[si], constB], writes=[pbB[7]], nowaw=True)
        else:
            Gt, gi_, half = rowslots[j]
            for b in range(NB):
                bank = (j + b) % 4
                mm_group([(pb[bank][:, :], screp[:, b, kc, :], slot(si)[:, kc, :], kc == 0, kc == 7) for kc in range(8)],
                         reads=[wB[si], setupB], writes=[pbB[bank]])
                sc.op("dve", lambda e: e.tensor_tensor(out=Gt[:, b, half * 512:(half + 1) * 512], in0=pb[bank][:, :],
                                                        in1=rows_bc[:, gi_ * 1024 + half * 512:gi_ * 1024 + (half + 1) * 512], op=ALU.add),
                      reads=[pbB[bank], setupB], writes=[GB], nowaw=True)
                sc.op("dve", lambda e: e.tensor_tensor(out=Gt[:, b, half * 512:(half + 1) * 512], in0=Gt[:, b, half * 512:(half + 1) * 512],
                                                        in1=rows_bc[:, 2048 + gi_ * 1024 + half * 512:2048 + gi_ * 1024 + (half + 1) * 512], op=ALU.mult),
                      reads=[GB, setupB], writes=[GB])
    if stop_after == "s3":
        return finish()
    bchunk = {0: 0, 1: 8, 2: 24, 3: 32}
    mp = modps[:, 0:32 * NB].rearrange("p (v k b) -> p v k b", v=4, k=8)
    for b in range(NB):
        for v in range(4):
            sc.op("dve", lambda e: e.tensor_tensor(out=AB[:, v, :, b], in0=mp[:, v, :, b], in1=bcol[:, bchunk[v]:bchunk[v] + 8], op=ALU.add),
                  reads=[pbB[7], setupB], writes=[ABB], nowaw=True)
        for v, gp in ((1, gpre[0]), (3, gpre[1])):
            sc.op("dve", lambda e: e.scalar_tensor_tensor(out=AB[:, v, :, b], in0=AB[:, v, :, b], scalar=1.0, in1=gp, op0=ALU.add, op1=ALU.mult),
                  reads=[ABB, setupB], writes=[ABB])
    dbg("AB", AB[:], [128, 4, 8, NB], F32, [ABB])
    dbg("G1", G1[:], [128, NB, 1024], F32, [GB])
    dbg("G2", G2[:], [128, NB, 1024], F32, [GB])
    fence([qBf] + uB + ptB + poolB, [setupB])

    bank_rr = [0]

    def next_bank(choices):
        b = choices[bank_rr[0] % len(choices)]
        bank_rr[0] += 1
        return b

    def rstd_chain(ssq_aps, tmp_ap, out_ap, rbufs, tbuf, obuf):
        sc.op("dve", lambda e: e.tensor_scalar(out=tmp_ap, in0=ssq_aps[0], scalar1=1.0 / D, scalar2=EPS, op0=ALU.mult, op1=ALU.add),
              reads=rbufs, writes=[tbuf])
        for extra in ssq_aps[1:]:
            sc.op("dve", lambda e: e.scalar_tensor_tensor(out=tmp_ap, in0=extra, scalar=1.0 / D, in1=tmp_ap, op0=ALU.mult, op1=ALU.add),
                  reads=rbufs + [tbuf], writes=[tbuf])
        sc.op("act", lambda e: e.activation(out=tmp_ap, in_=tmp_ap, func=AF.Sqrt), reads=[tbuf], writes=[tbuf])
        sc.op("dve", lambda e: e.reciprocal(out=out_ap, in_=tmp_ap), reads=[tbuf], writes=[obuf])

    def norm_transpose(t, b, rstd_ap, rbuf, vA, vB_, slot_i):
        xsl = slot_i % 2
        sc.op("dve", lambda e: e.tensor_scalar(out=xn[:, xsl, :], in0=xs[:, t, :], scalar1=rstd_ap, scalar2=None, op0=ALU.mult),
              reads=[xB[t], rbuf], writes=[xnB[xsl]])
        bank = 3 + (slot_i % 2)
        psT = pb[bank][:].bitcast(BF16)
        sc.begin("pe", [xnB[xsl], constB], [pbB[bank]])
        ins = None
        for kc in range(8):
            ins = nc.tensor.transpose(psT[:, kc * 128:(kc + 1) * 128], xn[:, xsl, kc * 128:(kc + 1) * 128], ident[:])
        sc.end("pe", ins, [xnB[xsl], constB], [pbB[bank]])
        for kc in range(8):
            evac_affine(HB[:, kc, t * 128:(t + 1) * 128], psT[:, kc * 128:(kc + 1) * 128], AB[:, vA, kc, b:b + 1], AB[:, vB_, kc, b:b + 1],
                        reads=[pbB[bank], ABB], writes=[hB[t]], eng="act" if slot_i % 2 == 0 else "dve")

    SSQA = stat[:, 0:8]
    TMPA = stat[:, 8:16]
    RSTDA = stat[:, 16:24]
    ssqABs = [Buf(f"ssqA{t}") for t in range(8)]
    tmpAB, rstdAB = Buf("tmpA"), Buf("rstdA")
    SSQD = stat[:, 24:40]
    TMPD = stat[:, 40:48]
    RSTDD = stat[:, 48:56]
    SSQ2 = stat[:, 56:64]
    TMP2 = stat[:, 64:72]
    RSTD2 = stat[:, 72:80]
    SSQF = stat[:, 80:96]
    TMPF = stat[:, 96:104]
    RSTDF = stat[:, 104:112]
    ssqDB = [[Buf(f"ssqD{t}_{c}") for c in range(2)] for t in range(8)]
    tmpDsB = [Buf(f"tmpDs{t}") for t in range(8)]
    rstdDB = [Buf(f"rstdD{t}") for t in range(8)]
    ssq2B = [Buf(f"ssq2{t}") for t in range(8)]
    tmp2B = [Buf(f"tmp2{t}") for t in range(8)]
    rstd2B = [Buf(f"rstd2{t}") for t in range(8)]
    ssqFB = [[Buf(f"ssqF{t}_{c}") for c in range(2)] for t in range(8)]
    tmpFB = [Buf(f"tmpF{t}") for t in range(8)]
    rstdFB = [Buf(f"rstdF{t}") for t in range(8)]

    chunks = [(0, 1), (0, 0), (1, 1), (1, 0)][:nchunks]
    if stop_after == "setup":
        chunks = []
    for ci, (s, half) in enumerate(chunks):
        b = s
        r0 = half * 1024
        gt0 = r0 // 128
        first = (ci == 0)

        fence(hB, stgB)
        fence([qBf] + uB + ptB + poolB, [actB] + sgB + tmpDB)
        for t in range(8):
            sc.dma("sp", xs[:, t, :], x_d[s, r0 + t * 128:r0 + (t + 1) * 128, :], dx[t], writes=[xB[t]])
        for si in range(4):
            load_w(si, wview(win_d, si * 512, (si + 1) * 512))
        if stop_after == "A0":
            continue
        for t in range(8):
            jk, jb_ = next_junk()
            sc.op("act", lambda e: e.activation(out=jk, in_=xs[:, t, :], func=AF.Square, accum_out=SSQA[:, t:t + 1]),
                  reads=[xB[t]], writes=[jb_, ssqABs[t]])
        if stop_after == "A1":
            continue
        rstd_chain([SSQA], TMPA, RSTDA, ssqABs, tmpAB, rstdAB)
        if stop_after == "A2":
            continue
        for t in range(8):
            norm_transpose(t, b, RSTDA[:, t:t + 1], rstdAB, 1, 0, t)
        if first:
            dbg("hT", HB[:], [128, 8, 1024], BF16, hB)

        if stop_after == "A":
            continue
        pbanks = [0, 1, 2, 5, 6]
        sc.op("pool", lambda e: e.memset(qA[64:128, :, :], 0.0), writes=[qBf])
        sc.op("pool", lambda e: e.memset(qB[0:64, :, :], 0.0), writes=[qBf], nowaw=True)
        for j in range(4):
            for tc in range(2):
                bank = next_bank(pbanks)
                mm_group([(pb[bank][:, :], slot(0)[:, kc, j * 128:(j + 1) * 128], HB[:, kc, tc * 512:(tc + 1) * 512], kc == 0, kc == 7)
                          for kc in range(8)], reads=[wB[0]] + hB[tc * 4:(tc + 1) * 4], writes=[pbB[bank]])
                qe = "act" if (j * 2 + tc) % 2 == 0 else "dve"
                evac_copy(qA[0:64, j, tc * 512:(tc + 1) * 512], pb[bank][0:64, :], reads=[pbB[bank]], writes=[qBf], eng=qe)
                evac_copy(qB[64:128, j, tc * 512:(tc + 1) * 512], pb[bank][64:128, :], reads=[pbB[bank]], writes=[qBf], eng=qe)
        for j in range(4):
            for tc in range(2):
                bank = next_bank(pbanks)
                mm_group([(pb[bank][:, :], slot(1)[:, kc, j * 128:(j + 1) * 128], HB[:, kc, tc * 512:(tc + 1) * 512], kc == 0, kc == 7)
                          for kc in range(8)], reads=[wB[1]] + hB[tc * 4:(tc + 1) * 4], writes=[pbB[bank]])
                evac_copy(kT[:, j, r0 + tc * 512:r0 + (tc + 1) * 512], pb[bank][:, :], reads=[pbB[bank]], writes=[kB[half]])
        for t in range(8):
            bank = next_bank(pbanks)
            mm_group([(pb[bank][:, :], HB[:, kc, t * 128:(t + 1) * 128], slot(2)[:, kc, :], kc == 0, kc == 7) for kc in range(8)],
                     reads=[wB[2], hB[t]], writes=[pbB[bank]])
            evac_copy(vv[:, gt0 + t, :], pb[bank][:, :], reads=[pbB[bank]], writes=[vB[half]])
        for g in range(4):
            for tc in range(2):
                bank = next_bank(pbanks)
                mm_group([(pb[bank][:, :], slot(3)[:, kc, g * 128:(g + 1) * 128], HB[:, kc, tc * 512:(tc + 1) * 512], kc == 0, kc == 7)
                          for kc in range(8)], reads=[wB[3]] + hB[tc * 4:(tc + 1) * 4], writes=[pbB[bank]])
                evac_copy(uT[:, g, tc * 512:(tc + 1) * 512], pb[bank][:, :], reads=[pbB[bank]], writes=[uB[g]])
        if first:
            dbg("qA", qA[:], [128, 4, 1024], BF16, [qBf])
            dbg("qB", qB[:], [128, 4, 1024], BF16, [qBf])
            dbg("kT", kT[:, :, 1024:2048], [128, 4, 1024], BF16, kB)
            dbg("vv", vv[:, 8:16, :], [128, 8, 512], BF16, vB)
            dbg("uT", uT[:, :, 0:1024], [128, 4, 1024], F32, uB)
        for chh in range(2):
            load_w(chh, wview(wout_d, chh * 512, (chh + 1) * 512))

        if stop_after == "B":
            continue
        for g in range(4):
            win = 2 << g
            if half == 1:
                sc.op("dve", lambda e: e.memset(uT[:, g, 1024:1040], 0.0), writes=[uB[g]], nowaw=True)
                sc.op("dve", lambda e: e.tensor_copy(out=uhalo[:, g, :], in_=uT[:, g, 0:16]), reads=[uB[g]], writes=[uhB], nowaw=True)
            else:
                sc.op("dve", lambda e: e.tensor_copy(out=uT[:, g, 1024:1040], in_=uhalo[:, g, :]), reads=[uhB], writes=[uB[g]], nowaw=True)
            src, srcB = uT[:, g, :], uB[g]
            n, sh, k = 1040, 1, 0
            while sh < win:
                n -= sh
                dst, dstB = (pt1, ptB[0]) if k % 2 == 0 else (pt2, ptB[1])
                sc.op("dve", lambda e: e.tensor_tensor(out=dst[:, 0:n], in0=src[:, 0:n], in1=src[:, sh:sh + n], op=ALU.add),
                      reads=[srcB], writes=[dstB])
                src, srcB = dst[:, :], dstB
                sh *= 2
                k += 1
            if half == 1:
                sc.op("dve", lambda e: e.tensor_tensor(out=src[:, 1009:1024], in0=src[:, 1009:1024], in1=corr[:, g, 0:15], op=ALU.mult),
                      reads=[srcB, constB], writes=[srcB])
            ps_ = g % 2
            sc.op("dve", lambda e: e.scalar_tensor_tensor(out=pooled[:, ps_, :], in0=src[:, 0:1024], scalar=1.0 / win, in1=uT[:, g, 0:1024],
                                                           op0=ALU.mult, op1=ALU.subtract), reads=[srcB, uB[g]], writes=[poolB[ps_]])
            for tc in range(2):
                bank = next_bank(pbanks)
                mm_group([(pb[bank][:, :], wpool[:, g, :], pooled[:, ps_, tc * 512:(tc + 1) * 512], True, True)],
                         reads=[setupB, poolB[ps_]], writes=[pbB[bank]])
                evac_affine(HB[:, 4 + g, tc * 512:(tc + 1) * 512], pb[bank][:, :], pscale[:, g:g + 1], None,
                            reads=[pbB[bank], setupB], writes=hB[tc * 4:(tc + 1) * 4], eng="act" if tc == 0 else "dve")

        if stop_after == "P":
            continue
        items = [(qb, h) for qb in range(8) for h in range(8)]
        zbanks = [0, 1, 2]
        zc = [0]

        def att_front(n):
            qb, h = items[n]
            gi = gt0 + qb
            nkb = 16 - gi
            L = nkb * 128
            sl = n % NGI
            qsrc = qA if h % 2 == 0 else qB
            j = h // 2
            nch = (L + 511) // 512
            for c in range(nch):
                ncl = min(512, L - c * 512)
                bank = zbanks[zc[0] % 3]
                zc[0] += 1
                mms = [(pb[bank][:, 0:ncl], qsrc[:, j, qb * 128:(qb + 1) * 128], kT[:, j, gi * 128 + c * 512:gi * 128 + c * 512 + ncl], True, c != 0)]
                if c == 0:
                    mms.append((pb[bank][:, 0:128], ident[:], maskneg[:], False, True))
                mm_group(mms, reads=[qBf, kB[0], kB[1], constB], writes=[pbB[bank]])
                sc.op("act", lambda e: e.activation(out=gI[:, sl, 1 + c * 512:1 + c * 512 + ncl], in_=pb[bank][:, 0:ncl], func=AF.Sigmoid, scale=-0.125),
                      reads=[pbB[bank]], writes=[gIB[sl]], nowaw=(c != 0))
            sc.op("dve", lambda e: e.tensor_tensor_scan(out=gI[:, sl, 1:L + 1], data0=gI[:, sl, 1:L + 1], data1=gI[:, sl, 1:L + 1],
                                                         initial=1.0, op0=ALU.mult, op1=ALU.min), reads=[gIB[sl]], writes=[gIB[sl]])
            sw = n % NWD
            Lp = (int(L * POOL_FRAC) // 2) * 2
            if Lp > 0:
                sc.op("pool", lambda e: e.tensor_tensor(out=wdf[:, sw, 0:Lp], in0=gI[:, sl, 0:Lp], in1=gI[:, sl, 1:Lp + 1], op=ALU.subtract),
                      reads=[gIB[sl]], writes=[wdB[sw]])
            sc.op("dve", lambda e: e.tensor_tensor(out=wdf[:, sw, Lp:L], in0=gI[:, sl, Lp:L], in1=gI[:, sl, Lp + 1:L + 1], op=ALU.subtract),
                  reads=[gIB[sl]], writes=[wdB[sw]], nowaw=(Lp > 0))

        def att_back(n):
            qb, h = items[n]
            gi = gt0 + qb
            nkb = 16 - gi
            sw = n % NWD
            st = n % 2
            j = h // 2
            kb = 0
            grp = 0
            while kb < nkb:
                ng = min(8, nkb - kb)
                bank = 3 + (grp + n) % 2
                psT = pb[bank][:].bitcast(BF16)
                sc.begin("pe", [wdB[sw], constB], [pbB[bank]])
                ins = None
                for i in range(ng):
                    ins = nc.tensor.transpose(psT[:, i * 128:(i + 1) * 128], wdf[:, sw, (kb + i) * 128:(kb + i + 1) * 128], ident[:])
                sc.end("pe", ins, [wdB[sw], constB], [pbB[bank]])
                sc.op("act", lambda e: e.activation(out=wT[:, st, kb * 128:(kb + ng) * 128], in_=psT[:, 0:ng * 128], func=AF.Copy),
                      reads=[pbB[bank]], writes=[wTB[st]], nowaw=(kb != 0))
                kb += ng
                grp += 1
            bank = 5 + (n % 2)
            mm_group([(pb[bank][:, 0:128], vv[:, gi + k2, j * 128:(j + 1) * 128], wT[:, st, k2 * 128:(k2 + 1) * 128], k2 == 0, k2 == nkb - 1)
                      for k2 in range(nkb)], reads=[vB[0], vB[1], wTB[st]], writes=[pbB[bank]])
            ph = (h % 2) * 64
            evac_copy(HB[ph:ph + 64, j, qb * 128:(qb + 1) * 128], pb[bank][ph:ph + 64, 0:128], reads=[pbB[bank]], writes=[hB[qb]],
                      eng="act")

        fence(gIB + wdB + wTB, uB + ptB + poolB)
        for sl in range(NGI):
            sc.op("dve", lambda e: e.memset(gI[:, sl, 0:1], 1.0), reads=[], writes=[gIB[sl]])
        NI = len(items)
        for n in range(NI + LA):
            if n < NI:
                att_front(n)
            if n >= LA:
                att_back(n - LA)
        if first:
            dbg("mixT", HB[:], [128, 8, 1024], BF16, hB)

        if stop_after == "C":
            continue
        load_w(2, wview(wgate_d, 0, 512))
        load_w(3, wview(wup_d, 0, 512))

        fence(tmpDB, gIB + wdB + wTB)
        dbank = [(0, 1), (2, 5), (6, 7)]

        def d_fin(t):
            banks = dbank[t % 3]
            rstd_chain([SSQD[:, 2 * t:2 * t + 1], SSQD[:, 2 * t + 1:2 * t + 2]], TMPD[:, t:t + 1], RSTDD[:, t:t + 1], ssqDB[t], tmpDsB[t], rstdDB[t])
            for chh in range(2):
                ts_ = chh
                sc.op("dve", lambda e: e.scalar_tensor_tensor(out=tmpD[:, ts_, :], in0=pb[banks[chh]][:, :], scalar=RSTDD[:, t:t + 1],
                                                               in1=G1[:, b, chh * 512:(chh + 1) * 512], op0=ALU.mult, op1=ALU.mult),
                      reads=[pbB[banks[chh]], rstdDB[t], GB], writes=[tmpDB[ts_]])
                sc.op("pool", lambda e: e.tensor_tensor(out=xs[:, t, chh * 512:(chh + 1) * 512], in0=xs[:, t, chh * 512:(chh + 1) * 512],
                                                         in1=tmpD[:, ts_, :], op=ALU.add), reads=[tmpDB[ts_], xB[t]], writes=[xB[t]])

        for t in range(8):
            banks = dbank[t % 3]
            for chh in range(2):
                mm_group([(pb[banks[chh]][:, :], HB[:, kc, t * 128:(t + 1) * 128], slot(chh)[:, kc, :], kc == 0, kc == 7) for kc in range(8)],
                         reads=[wB[chh], hB[t]], writes=[pbB[banks[chh]]])
                jk, jb_ = next_junk()
                sc.op("act", lambda e: e.activation(out=jk[:, 0:512], in_=pb[banks[chh]][:, :], func=AF.Square,
                                                     accum_out=SSQD[:, 2 * t + chh:2 * t + chh + 1]),
                      reads=[pbB[banks[chh]]], writes=[jb_, ssqDB[t][chh]])
            if t >= 1:
                d_fin(t - 1)
        d_fin(7)
        for t in range(8):
            jk, jb_ = next_junk()
            sc.op("act", lambda e: e.activation(out=jk, in_=xs[:, t, :], func=AF.Square, accum_out=SSQ2[:, t:t + 1]),
                  reads=[xB[t]], writes=[jb_, ssq2B[t]])
        rstd_chain([SSQ2], TMP2, RSTD2, ssq2B, tmp2B[0], rstd2B[0])
        for t in range(8):
            norm_transpose(t, b, RSTD2[:, t:t + 1], rstd2B[0], 3, 2, t)
        if first:
            dbg("x1", xs[:], [128, 8, 1024], F32, xB)
            dbg("h2T", HB[:], [128, 8, 1024], BF16, hB)

        if stop_after == "D":
            continue
        fence([actB] + sgB, [qBf] + gIB + wdB + wTB + tmpDB + uB + ptB + poolB)
        nfs = 6
        gu_banks = [(0, 1), (2, 5)]
        it = 0
        for fs in range(nfs):
            sg_, su_ = (2, 3) if fs % 2 == 0 else (0, 1)
            nfl = min(4, NF - fs * 4)
            if fs + 1 < nfs:
                ng_, nu_ = (0, 1) if fs % 2 == 0 else (2, 3)
                c0 = (fs + 1) * 512
                c1 = min(DFF, c0 + 512)
                load_w(ng_, wview(wgate_d, c0, c1), c1 - c0)
                load_w(nu_, wview(wup_d, c0, c1), c1 - c0)
            for f4 in range(nfl):
                f = fs * 4 + f4
                for tc in range(2):
                    gb_, ub_ = gu_banks[it % 2]
                    sgs = it % 2
                    it += 1
                    mm_group([(pb[gb_][:, :], slot(sg_)[:, kc, f4 * 128:(f4 + 1) * 128], HB[:, kc, tc * 512:(tc + 1) * 512], kc == 0, kc == 7)
                              for kc in range(8)], reads=[wB[sg_]] + hB[tc * 4:(tc + 1) * 4], writes=[pbB[gb_]])
                    mm_group([(pb[ub_][:, :], slot(su_)[:, kc, f4 * 128:(f4 + 1) * 128], HB[:, kc, tc * 512:(tc + 1) * 512], kc == 0, kc == 7)
                              for kc in range(8)], reads=[wB[su_]] + hB[tc * 4:(tc + 1) * 4], writes=[pbB[ub_]])
                    sc.op("act", lambda e: e.activation(out=sg[:, sgs, :], in_=pb[gb_][:, :], func=AF.Silu), reads=[pbB[gb_]], writes=[sgB[sgs]])
                    sc.op("dve", lambda e: e.tensor_tensor(out=actT[:, f, tc * 512:(tc + 1) * 512], in0=sg[:, sgs, :], in1=pb[ub_][:, :], op=ALU.mult),
                          reads=[sgB[sgs], pbB[ub_]], writes=[actB], nowaw=True)
        if first:
            dbg("actT", actT[:], [128, NF, 1024], BF16, [actB])

        if stop_after == "E":
            continue
        fence(tmpDB, sgB)
        wdv = wdown_d.rearrange("(f p) n -> p f n", p=128)
        dslots = [(0, 8), (8, 8), (16, 6)]
        useq = 0
        for chh in range(2):
            for di, (f0, nfd) in enumerate(dslots):
                si = useq % 4
                useq += 1
                sc.dma("pool", slot(si)[:, 0:nfd, :], wdv[:, f0:f0 + nfd, chh * 512:(chh + 1) * 512], dw[si], writes=[wB[si]])
                for f8 in range(nfd):
                    f = f0 + f8
                    for t in range(8):
                        mm_group([(pb[t][:, :], actT[:, f, t * 128:(t + 1) * 128], slot(si)[:, f8, :], f == 0, f == NF - 1)],
                                 reads=[wB[si], actB], writes=[pbB[t]])
            for t in range(8):
                jk, jb_ = next_junk()
                sc.op("act", lambda e: e.activation(out=jk[:, 0:512], in_=pb[t][:, :], func=AF.Square,
                                                     accum_out=SSQF[:, 2 * t + chh:2 * t + chh + 1]),
                      reads=[pbB[t]], writes=[jb_, ssqFB[t][chh]])
                if chh == 0:
                    sc.op("dve", lambda e: e.tensor_copy(out=stage[:, t, :], in_=pb[t][:, :]),
                          reads=[pbB[t]], writes=[stgB[t]] + hB)
            if chh == 1:
                sq2 = SSQF.rearrange("p (t c) -> p t c", c=2)
                rstd_chain([sq2[:, :, 0], sq2[:, :, 1]], TMPF, RSTDF, [x_ for p_ in ssqFB for x_ in p_], tmpFB[0], rstdFB[0])
                for t in range(8):
                    for c2 in range(2):
                        src_ap = stage[:, t, :] if c2 == 0 else pb[t][:, :]
                        srcbuf = stgB[t] if c2 == 0 else pbB[t]
                        ts_ = c2
                        sc.op("dve", lambda e: e.scalar_tensor_tensor(out=tmpD[:, ts_, :], in0=src_ap, scalar=RSTDF[:, t:t + 1],
                                                                       in1=G2[:, b, c2 * 512:(c2 + 1) * 512], op0=ALU.mult, op1=ALU.mult),
                              reads=[srcbuf, rstdFB[0], GB], writes=[tmpDB[ts_]])
                        sc.op("pool", lambda e: e.tensor_tensor(out=xs[:, t, c2 * 512:(c2 + 1) * 512], in0=xs[:, t, c2 * 512:(c2 + 1) * 512],
                                                                 in1=tmpD[:, ts_, :], op=ALU.add), reads=[tmpDB[ts_], xB[t]], writes=[xB[t]])
                    sc.dma("sp", out_d[s, r0 + t * 128:r0 + (t + 1) * 128, :], xs[:, t, :], dx[t], reads=[xB[t]])

    return finish()


def make_in_maps(x, c, w_cond, b_cond, g_mix_pre, g_mix_post, w_in, w_pool, pool_scale, w_out,
                 g_ffn_pre, g_ffn_post, w_gate, w_up, w_down):
    f = lambda a: np.ascontiguousarray(np.asarray(a, dtype=np.float32))
    x = f(x)
    c = f(c)
    xr = x[:, ::-1, :]
    bc = f(b_cond)[0]
    colv = lambda v: v.reshape(-1, 128).T
    rows = np.concatenate([bc[2048:3072], bc[5120:6144], f(g_mix_post)[0], f(g_ffn_post)[0]])[None, :]
    shared = dict(rows=f(rows), w_cond=f(w_cond)[0], w_in=f(w_in)[0], w_pool=f(w_pool)[0], w_out=f(w_out)[0],
                  w_gate=f(w_gate)[0], w_up=f(w_up)[0], w_down=f(w_down)[0])
    maps = []
    for i in range(NCORES):
        cb = c[i * NB:(i + 1) * NB]
        cT = cb.reshape(NB, 8, 128).transpose(2, 1, 0).reshape(128, 8 * NB)
        cols = np.concatenate([cT, colv(bc), colv(f(g_mix_pre)[0]), colv(f(g_ffn_pre)[0]), colv(f(pool_scale)[0])], axis=1)
        m = dict(shared)
        m["x"] = f(xr[i * NB:(i + 1) * NB])
        m["cols"] = f(cols)
        maps.append(m)
    return maps


_NC_CACHE = {}


def kernel(**inputs):
    if "nc" not in _NC_CACHE:
        _NC_CACHE["nc"] = build_nc()[0]
    nc = _NC_CACHE["nc"]
    maps = make_in_maps(**inputs)
    res = run_bass_kernel_spmd(nc, maps, core_ids=list(range(NCORES)))
    outs = [np.asarray(r["out"]) for r in res.results]
    full = np.concatenate(outs, axis=0)[:, ::-1, :]
    return np.ascontiguousarray(full.astype(np.float32))
```
